# Optimizing a Trainium2 kernel written in Bass

```python
import jax, jax.numpy as jnp
from jax import lax
import numpy as np

D_MODEL = 1024
BATCH = 8
SEQ = 4096
DEPTH = 2

GRID_W = 64
CTX_LEN = 256
N_MIXERS = 2
HEAD_DIM = 64
NORM_EPS = 1e-6
ATTN_SCALE = HEAD_DIM ** -0.5
A_HEADS = 16
A_KV_HEADS = 4
A_GROUP = A_HEADS // A_KV_HEADS
A_WIDTH = A_HEADS * HEAD_DIM
A_KV_WIDTH = A_KV_HEADS * HEAD_DIM
A_IN = 2 * A_WIDTH + 2 * A_KV_WIDTH
ROPE_THETA = 10000.0
ROPE_AXIS_DIM = HEAD_DIM // 2
ROPE_HALF = ROPE_AXIS_DIM // 2
Q_BLOCK = 128
B_HEADS = 16
B_WIDTH = B_HEADS * HEAD_DIM
B_IN = 4 * B_WIDTH
WIN_R = 8
WIN_C = 16
N_A_LAYERS = (DEPTH + N_MIXERS - 1) // N_MIXERS
N_B_LAYERS = DEPTH // N_MIXERS

kernel_name = "hybrid_gqa_natten_prefix_dit"


def rms_norm(x, g):
    xf = x.astype(jnp.float32)
    y = xf * lax.rsqrt(jnp.mean(xf * xf, axis=-1, keepdims=True) + NORM_EPS)
    return (y * g.astype(jnp.float32)).astype(x.dtype)


def axial_rope_tables(t_len):
    pos = jnp.arange(t_len, dtype=jnp.int32)
    row = (pos // GRID_W).astype(jnp.float32)
    col = (pos % GRID_W).astype(jnp.float32)
    inv = ROPE_THETA ** (-jnp.arange(0, ROPE_AXIS_DIM, 2, dtype=jnp.float32) / ROPE_AXIS_DIM)
    ang = jnp.stack([row[:, None] * inv, col[:, None] * inv], axis=1)
    return jnp.cos(ang), jnp.sin(ang)


def apply_axial_rope(x, cos, sin):
    shp = x.shape
    xs = x.reshape(shp[0], shp[1], shp[2], 2, 2, ROPE_HALF)
    x1 = xs[..., 0, :]
    x2 = xs[..., 1, :]
    cc = cos.astype(x.dtype)[None, :, None]
    ss = sin.astype(x.dtype)[None, :, None]
    out = jnp.stack([x1 * cc - x2 * ss, x2 * cc + x1 * ss], axis=-2)
    return out.reshape(shp)


def gqa_axial_mixer(hx, hc, w_in, q_g, k_g, w_out, cos, sin, need_ctx_out):
    b, t, _ = hx.shape
    lc = hc.shape[1]
    px = hx @ w_in
    q = px[..., :A_WIDTH].reshape(b, t, A_HEADS, HEAD_DIM)
    k = px[..., A_WIDTH:A_WIDTH + A_KV_WIDTH].reshape(b, t, A_KV_HEADS, HEAD_DIM)
    v = px[..., A_WIDTH + A_KV_WIDTH:A_WIDTH + 2 * A_KV_WIDTH].reshape(b, t, A_KV_HEADS, HEAD_DIM)
    z = px[..., A_WIDTH + 2 * A_KV_WIDTH:]
    q = apply_axial_rope(rms_norm(q, q_g), cos, sin)
    k = apply_axial_rope(rms_norm(k, k_g), cos, sin)
    pkv = hc @ w_in[:, A_WIDTH:A_WIDTH + 2 * A_KV_WIDTH]
    kc = rms_norm(pkv[..., :A_KV_WIDTH].reshape(b, lc, A_KV_HEADS, HEAD_DIM), k_g)
    vc = pkv[..., A_KV_WIDTH:].reshape(b, lc, A_KV_HEADS, HEAD_DIM)
    k_all = jnp.concatenate([kc, k], axis=1)
    v_all = jnp.concatenate([vc, v], axis=1)
    n_blk = t // Q_BLOCK
    qb = q.reshape(b, n_blk, Q_BLOCK, A_KV_HEADS, A_GROUP, HEAD_DIM).transpose(1, 0, 2, 3, 4, 5)

    def block(qi):
        s = jnp.einsum('bqkgd,bskd->bkgqs', qi, k_all).astype(jnp.float32) * ATTN_SCALE
        p = jax.nn.softmax(s, axis=-1).astype(v_all.dtype)
        return jnp.einsum('bkgqs,bskd->bqkgd', p, v_all)

    o = lax.map(block, qb)
    o = o.transpose(1, 0, 2, 3, 4, 5).reshape(b, t, A_WIDTH)
    yx = (o * jax.nn.silu(z)) @ w_out
    if not need_ctx_out:
        return yx, None
    qc = rms_norm((hc @ w_in[:, :A_WIDTH]).reshape(b, lc, A_HEADS, HEAD_DIM), q_g)
    qc = qc.reshape(b, lc, A_KV_HEADS, A_GROUP, HEAD_DIM)
    zc = hc @ w_in[:, A_WIDTH + 2 * A_KV_WIDTH:]
    sc = jnp.einsum('bqkgd,bskd->bkgqs', qc, kc).astype(jnp.float32) * ATTN_SCALE
    pc = jax.nn.softmax(sc, axis=-1).astype(vc.dtype)
    oc = jnp.einsum('bkgqs,bskd->bqkgd', pc, vc).reshape(b, lc, A_WIDTH)
    yc = (oc * jax.nn.silu(zc)) @ w_out
    return yx, yc


def neighbourhood_mixer(hx, hc, w_in, rpb, w_out, need_ctx_out):
    b, t, _ = hx.shape
    lc = hc.shape[1]
    rows = t // GRID_W
    wr = min(WIN_R, rows)
    px = hx @ w_in
    q = px[..., :B_WIDTH].reshape(b, rows, GRID_W, B_HEADS, HEAD_DIM)
    k = px[..., B_WIDTH:2 * B_WIDTH].reshape(b, rows, GRID_W, B_HEADS, HEAD_DIM)
    v = px[..., 2 * B_WIDTH:3 * B_WIDTH].reshape(b, rows, GRID_W, B_HEADS, HEAD_DIM)
    z = px[..., 3 * B_WIDTH:]
    pkv = hc @ w_in[:, B_WIDTH:3 * B_WIDTH]
    kc = pkv[..., :B_WIDTH].reshape(b, lc, B_HEADS, HEAD_DIM)
    vc = pkv[..., B_WIDTH:].reshape(b, lc, B_HEADS, HEAD_DIM)
    qcol = np.arange(GRID_W)
    c0 = np.clip(qcol - WIN_C // 2, 0, GRID_W - WIN_C)
    col_idx = c0[:, None] + np.arange(WIN_C)[None, :]
    dc_idx = col_idx - qcol[:, None] + (WIN_C - 1)
    n_nb = wr * WIN_C

    def row_block(r):
        r0 = jnp.clip(r - wr // 2, 0, rows - wr)
        kb = lax.dynamic_slice_in_dim(k, r0, wr, axis=1)
        vb = lax.dynamic_slice_in_dim(v, r0, wr, axis=1)
        kw = kb[:, :, col_idx].transpose(0, 2, 1, 3, 4, 5).reshape(b, GRID_W, n_nb, B_HEADS, HEAD_DIM)
        vw = vb[:, :, col_idx].transpose(0, 2, 1, 3, 4, 5).reshape(b, GRID_W, n_nb, B_HEADS, HEAD_DIM)
        qr = lax.dynamic_index_in_dim(q, r, axis=1, keepdims=False)
        dr_idx = r0 + jnp.arange(wr) - r + (WIN_R - 1)
        bias = rpb[:, dr_idx][:, :, dc_idx]
        bias = bias.transpose(0, 2, 1, 3).reshape(B_HEADS, GRID_W, n_nb).astype(jnp.float32)
        s_nb = jnp.einsum('bqhd,bqkhd->bhqk', qr, kw).astype(jnp.float32) * ATTN_SCALE + bias
        s_cx = jnp.einsum('bqhd,bshd->bhqs', qr, kc).astype(jnp.float32) * ATTN_SCALE
        p = jax.nn.softmax(jnp.concatenate([s_nb, s_cx], axis=-1), axis=-1).astype(vw.dtype)
        return (jnp.einsum('bhqk,bqkhd->bqhd', p[..., :n_nb], vw)
                + jnp.einsum('bhqs,bshd->bqhd', p[..., n_nb:], vc))

    o = lax.map(row_block, jnp.arange(rows))
    o = o.transpose(1, 0, 2, 3, 4).reshape(b, t, B_WIDTH)
    yx = (o * jax.nn.silu(z)) @ w_out
    if not need_ctx_out:
        return yx, None
    qc = (hc @ w_in[:, :B_WIDTH]).reshape(b, lc, B_HEADS, HEAD_DIM)
    zc = hc @ w_in[:, 3 * B_WIDTH:]
    sc = jnp.einsum('bqhd,bshd->bhqs', qc, kc).astype(jnp.float32) * ATTN_SCALE
    pc = jax.nn.softmax(sc, axis=-1).astype(vc.dtype)
    oc = jnp.einsum('bhqs,bshd->bqhd', pc, vc).reshape(b, lc, B_WIDTH)
    yc = (oc * jax.nn.silu(zc)) @ w_out
    return yx, yc


def setup_inputs(seed: int = 0) -> dict:
    key = jax.random.key(seed)
    ks = jax.random.split(key, 20)
    f32 = jnp.float32
    d = D_MODEL
    nrm = lambda k, shp, s: (jax.random.normal(k, shp, f32) * s)
    return {
        "x": nrm(ks[0], (BATCH, SEQ, d), 1.0),
        "c": nrm(ks[1], (BATCH, d), 1.0),
        "ctx": nrm(ks[2], (BATCH, CTX_LEN, d), 1.0),
        "c_ctx": nrm(ks[3], (d,), 1.0),
        "norm_g": 1.0 + nrm(ks[4], (DEPTH, d), 0.02),
        "w_mod": nrm(ks[5], (DEPTH, d, 3 * d), 0.5 * d ** -0.5),
        "b_mod": nrm(ks[6], (DEPTH, 3 * d), 0.01),
        "a_w_in": nrm(ks[7], (N_A_LAYERS, d, A_IN), d ** -0.5),
        "a_q_norm_g": 1.0 + nrm(ks[8], (N_A_LAYERS, HEAD_DIM), 0.02),
        "a_k_norm_g": 1.0 + nrm(ks[9], (N_A_LAYERS, HEAD_DIM), 0.02),
        "a_w_out": nrm(ks[10], (N_A_LAYERS, A_WIDTH, d), A_WIDTH ** -0.5),
        "b_w_in": nrm(ks[11], (N_B_LAYERS, d, B_IN), d ** -0.5),
        "b_rpb": nrm(ks[12], (N_B_LAYERS, B_HEADS, 2 * WIN_R - 1, 2 * WIN_C - 1), 0.02),
        "b_w_out": nrm(ks[13], (N_B_LAYERS, B_WIDTH, d), B_WIDTH ** -0.5),
        "final_norm_g": 1.0 + nrm(ks[14], (d,), 0.02),
    }


def reference(x, c, ctx, c_ctx, norm_g, w_mod, b_mod, a_w_in, a_q_norm_g, a_k_norm_g, a_w_out,
              b_w_in, b_rpb, b_w_out, final_norm_g):
    t = x.shape[1]
    cos, sin = axial_rope_tables(t)
    silu_c = jax.nn.silu(c)
    silu_cc = jax.nn.silu(c_ctx)
    ia = 0
    ib = 0
    for i in range(DEPTH):
        last = i == DEPTH - 1
        mod_x = (silu_c @ w_mod[i] + b_mod[i])[:, None, :]
        mod_c = silu_cc @ w_mod[i] + b_mod[i]
        sh_x, sc_x, g_x = jnp.split(mod_x, 3, axis=-1)
        sh_c, sc_c, g_c = jnp.split(mod_c, 3, axis=-1)
        hx = rms_norm(x, norm_g[i]) * (1.0 + sc_x) + sh_x
        hc = rms_norm(ctx, norm_g[i]) * (1.0 + sc_c) + sh_c
        if i % N_MIXERS == 0:
            yx, yc = gqa_axial_mixer(hx, hc, a_w_in[ia], a_q_norm_g[ia], a_k_norm_g[ia], a_w_out[ia],
                                     cos, sin, not last)
            ia += 1
        else:
            yx, yc = neighbourhood_mixer(hx, hc, b_w_in[ib], b_rpb[ib], b_w_out[ib], not last)
            ib += 1
        x = x + g_x * yx
        if not last:
            ctx = ctx + g_c * yc
    return rms_norm(x, final_norm_g)
```

```python
import numpy as np
from contextlib import ExitStack
import concourse.bass as bass
import concourse.mybir as mybir
from concourse.bass_utils import run_bass_kernel_spmd

F32 = mybir.dt.float32
BF16 = mybir.dt.bfloat16
AF = mybir.ActivationFunctionType
ALU = mybir.AluOpType

T = 4096
D = 1024
LC = 256
NEG = -30000.0
HEAD_PERM = [0, 4, 1, 5, 2, 6, 3, 7, 8, 12, 9, 13, 10, 14, 11, 15]
SEM_ROLL = 12000
NFILL = 512
NFILLN = 1
BG_EVERY = 3


class Sem:
    def __init__(self, handle, name):
        self.h = handle
        self.name = name
        self.count = 0


class Buf:
    __slots__ = ("name", "writers", "readers", "base_writers")

    def __init__(self, name):
        self.name = name
        self.writers = {}
        self.base_writers = {}
        self.readers = {}


class EngQ:
    def __init__(self, fw, name, eng):
        self.fw = fw
        self.name = name
        self.eng = eng
        self.sem = fw.new_sem("e_" + name)
        self.waited = {}
        self.nroll = 0

    def wait(self, tok):
        if tok is None:
            return
        sem, val = tok
        if self.waited.get(sem, 0) >= val:
            return
        self.eng.wait_ge(sem.h, val)
        self.waited[sem] = val


class Rot:
    def __init__(self, items):
        self.items = items
        self.i = 0

    def next(self):
        it = self.items[self.i % len(self.items)]
        self.i += 1
        return it


class FW:
    def __init__(self, nc, stack):
        self.nc = nc
        self.stack = stack
        self.all_sems = []
        self.E = {}
        for name, eng in (("pe", nc.tensor), ("act", nc.scalar), ("dve", nc.vector),
                          ("pool", nc.gpsimd), ("sp", nc.sync)):
            self.E[name] = EngQ(self, name, eng)

    def new_sem(self, name):
        h = self.stack.enter_context(self.nc.semaphore(name))
        s = Sem(h, name)
        self.all_sems.append(s)
        return s

    def _deps(self, q, reads, writes, skip_self, accum_w=False):
        for b in reads:
            for s, v in b.writers.items():
                if skip_self and s is q.sem:
                    continue
                q.wait((s, v))
        for b in writes:
            for s, v in b.readers.items():
                if skip_self and s is q.sem:
                    continue
                q.wait((s, v))
            for s, v in (b.base_writers if accum_w else b.writers).items():
                if skip_self and s is q.sem:
                    continue
                q.wait((s, v))

    @staticmethod
    def _commit(tok, reads, writes, accum_w=False):
        s, v = tok
        for b in reads:
            if b.readers.get(s, 0) < v:
                b.readers[s] = v
        for b in writes:
            if b.writers.get(s, 0) < v:
                b.writers[s] = v
            if not accum_w and b.base_writers.get(s, 0) < v:
                b.base_writers[s] = v

    def op(self, ename, build, reads=(), writes=(), skip_self=False, accum_w=False):
        q = self.E[ename]
        if q.sem.count >= SEM_ROLL:
            q.nroll += 1
            q.sem = self.new_sem("e_%s_%d" % (ename, q.nroll))
        self._deps(q, reads, writes, skip_self, accum_w)
        ins = build(q.eng)
        q.sem.count += 1
        ins.then_inc(q.sem.h, 1)
        tok = (q.sem, q.sem.count)
        self._commit(tok, reads, writes, accum_w)
        return tok

    def dma(self, ename, sem, out, in_, reads=(), writes=()):
        q = self.E[ename]
        self._deps(q, reads, writes, False)
        ins = q.eng.dma_start(out=out, in_=in_)
        sem.count += 16
        ins.then_inc(sem.h, 16)
        tok = (sem, sem.count)
        self._commit(tok, reads, writes)
        return tok

    def barrier(self):
        toks = [(s, s.count) for s in self.all_sems if s.count > 0]
        for q in self.E.values():
            for t in toks:
                q.wait(t)


def _l1_window_meta():
    r0 = lambda r: int(np.clip(r - 4, 0, 56))
    masks = []
    mask_of = {}
    key_to_idx = {}
    chunks = {}
    for qb in range(32):
        lo = r0(2 * qb) // 2
        hi = (r0(2 * qb + 1) + 7) // 2
        chunks[qb] = list(range(lo, hi + 1))
        for m in chunks[qb]:
            mk = np.zeros((128, 128), np.float32)
            for kr2 in range(2):
                for qr2 in range(2):
                    kr = 2 * m + kr2
                    qr = 2 * qb + qr2
                    ok = r0(qr) <= kr < r0(qr) + 8
                    if not ok:
                        mk[kr2 * 64:(kr2 + 1) * 64, qr2 * 64:(qr2 + 1) * 64] = NEG
            key = mk.tobytes()
            if key not in key_to_idx:
                key_to_idx[key] = len(masks)
                masks.append(mk)
            mask_of[(qb, m)] = key_to_idx[key]
    return chunks, mask_of, np.stack(masks, 0)


def _bias_tiles(rpb):
    H = rpb.shape[0]
    kc = np.arange(64)[:, None]
    qc = np.arange(64)[None, :]
    c0 = np.clip(np.arange(64) - 8, 0, 48)[None, :]
    colok = (kc >= c0) & (kc < c0 + 16)
    dc = np.clip(kc - qc + 15, 0, 30)
    tb = np.zeros((H, 7, 128, 128), np.float32)
    for di in range(7):
        d = di - 3
        for kr2 in range(2):
            for qr2 in range(2):
                a = 2 * d + kr2 - qr2 + 7
                if 0 <= a < 15:
                    blk = rpb[:, a][:, dc]
                else:
                    blk = np.zeros((H, 64, 64), np.float32)
                blk = np.where(colok[None], blk, np.float32(NEG))
                tb[:, di, kr2 * 64:(kr2 + 1) * 64, qr2 * 64:(qr2 + 1) * 64] = blk
    return tb


def _rope_tables():
    pos = np.arange(T)
    row = (pos // 64).astype(np.float32)
    col = (pos % 64).astype(np.float32)
    inv = (np.float32(10000.0) ** (-np.arange(0, 32, 2, dtype=np.float32) / np.float32(32))).astype(np.float32)
    C = np.zeros((128, T), np.float32)
    S = np.zeros((128, T), np.float32)
    RT = np.zeros((128, 128), np.float32)
    for p in range(128):
        d = p % 64
        axis = d // 32
        half = (d % 32) // 16
        f = d % 16
        ang = (row if axis == 0 else col) * inv[f]
        C[p] = np.cos(ang)
        S[p] = np.sin(ang) * (-1.0 if half == 0 else 1.0)
        partner = p + 16 if half == 0 else p - 16
        RT[partner, p] = 1.0
    return C, S, RT


def _prep(inputs):
    f = lambda k: np.ascontiguousarray(np.asarray(inputs[k], dtype=np.float32))
    sh = {}
    sh["w_mod"] = f("w_mod")
    bm = f("b_mod")
    sh["bmod_fm"] = np.ascontiguousarray(bm.reshape(2, 24, 128).transpose(2, 0, 1))
    sh["bmod_g"] = np.ascontiguousarray(bm[:, 2048:3072])
    sh["b_mod"] = bm
    sh["norm_g_fm"] = np.ascontiguousarray(f("norm_g").reshape(2, 8, 128).transpose(2, 0, 1))
    sh["fg"] = f("final_norm_g").reshape(1, D)
    hcols = np.concatenate([np.arange(64) + 64 * h for h in HEAD_PERM])
    a_in = f("a_w_in")[0]
    sh["w0"] = np.ascontiguousarray(np.concatenate(
        [a_in[:, :1024][:, hcols], a_in[:, 1024:1536], a_in[:, 1536:][:, hcols]], axis=1))
    sh["wout0"] = np.ascontiguousarray(f("a_w_out")[0][hcols, :])
    qg = f("a_q_norm_g")[0]
    kg = f("a_k_norm_g")[0]
    sh["qkg"] = np.ascontiguousarray(np.stack([np.tile(qg, 2), np.tile(kg, 2)], axis=1))
    C, S, RT = _rope_tables()
    sh["ropeC"] = C
    sh["ropeS"] = S
    sh["RT"] = RT
    sh["ident"] = np.eye(128, dtype=np.float32)
    blk = np.zeros((128, 128), np.float32)
    blk[:64, :64] = 1.0
    blk[64:, 64:] = 1.0
    sh["blk1"] = blk
    b_in = f("b_w_in")[0]
    w1 = np.zeros((2, D, 2048), np.float32)
    for p in range(2):
        for part in range(4):
            w1[p, :, part * 512:(part + 1) * 512] = b_in[:, part * 1024 + p * 512: part * 1024 + (p + 1) * 512]
    sh["w1"] = w1
    sh["wout1"] = f("b_w_out")[0]
    tb = _bias_tiles(f("b_rpb")[0])
    chunks, mask_of, masks = _l1_window_meta()
    combos = sorted({(m - qb + 3, mask_of[(qb, m)]) for qb in range(32) for m in chunks[qb]})
    combo_of = {c: i for i, c in enumerate(combos)}
    tbm = np.stack([np.where(masks[mi][None] < 0, np.float32(NEG), tb[:, di]) for di, mi in combos], axis=1)
    nm = len(combos)
    sh["tb"] = np.ascontiguousarray(tbm.reshape(2, 8, nm, 128, 128)[:, [0, 2, 4, 6, 1, 3, 5, 7]].transpose(0, 3, 2, 1, 4))
    mask_of = {k: combo_of[(k[1] - k[0] + 3, v)] for k, v in mask_of.items()}
    cc = f("c_ctx").reshape(8, 128).T
    per_core = []
    x = np.asarray(inputs["x"], dtype=np.float32)
    ctx = np.asarray(inputs["ctx"], dtype=np.float32)
    c = np.asarray(inputs["c"], dtype=np.float32)
    for b in range(x.shape[0]):
        cv = np.stack([c[b].reshape(8, 128).T, cc], axis=2)
        m = dict(sh)
        m["x"] = np.ascontiguousarray(x[b])
        m["ctx"] = np.ascontiguousarray(ctx[b])
        m["cvec"] = np.ascontiguousarray(cv)
        per_core.append(m)
    return per_core, (chunks, mask_of, nm)


def build(meta, debug=False, stop_after=None):
    chunks_of, mask_of, NM = meta
    nc = bass.Bass("TRN2", target_bir_lowering=False)

    def din(name, shape):
        return nc.dram_tensor(name, list(shape), F32, kind="ExternalInput").ap()

    x_d = din("x", [T, D])
    ctx_d = din("ctx", [LC, D])
    cvec_d = din("cvec", [128, 8, 2])
    wmod_d = din("w_mod", [2, D, 3072])
    bmfm_d = din("bmod_fm", [128, 2, 24])
    bmg_d = din("bmod_g", [2, D])
    bmod_d = din("b_mod", [2, 3072])
    modscr_d = nc.dram_tensor("modscr", [2, 2, 3072], F32).ap()
    b_modscr = Buf("modscr")
    ngfm_d = din("norm_g_fm", [128, 2, 8])
    fg_d = din("fg", [1, D])
    w0_d = din("w0", [D, 2560])
    wout0_d = din("wout0", [D, D])
    qkg_d = din("qkg", [128, 2])
    ropeC_d = din("ropeC", [128, T])
    ropeS_d = din("ropeS", [128, T])
    RT_d = din("RT", [128, 128])
    ident_d = din("ident", [128, 128])
    blk1_d = din("blk1", [128, 128])
    w1_d = din("w1", [2, D, 2048])
    wout1_d = din("wout1", [D, D])
    tb_d = din("tb", [2, 128, NM, 8, 128])
    out_d = nc.dram_tensor("out", [T, D], F32, kind="ExternalOutput").ap()
    scratch_kind = "ExternalOutput" if debug else "Internal"
    x1_d = nc.dram_tensor("x1", [T, D], F32, kind=scratch_kind).ap()
    x1b_d = nc.dram_tensor("x1b", [T, D], F32, kind=scratch_kind).ap()
    if debug:
        dbg_ctx1 = nc.dram_tensor("dbg_ctx1", [128, 2, D], F32, kind="ExternalOutput").ap()
        dbg_kt = nc.dram_tensor("dbg_kt", [128, 2, 4352], BF16, kind="ExternalOutput").ap()
        dbg_v = nc.dram_tensor("dbg_v", [128, 34, 4, 128], BF16, kind="ExternalOutput").ap()
        dbg_ab = nc.dram_tensor("dbg_ab", [128, 2, 4, 8], F32, kind="ExternalOutput").ap()
        dbg_gb = nc.dram_tensor("dbg_gb", [128, 3, D], F32, kind="ExternalOutput").ap()
        dbg_og1 = nc.dram_tensor("dbg_og1", [128, 4, 512], BF16, kind="ExternalOutput").ap()
        dbg_qt1 = nc.dram_tensor("dbg_qt1", [128, 4, 512], BF16, kind="ExternalOutput").ap()
        dbg_zt1 = nc.dram_tensor("dbg_zt1", [128, 4, 512], BF16, kind="ExternalOutput").ap()
        dbg_ktr = nc.dram_tensor("dbg_ktr", [128, 4, 2048], BF16, kind="ExternalOutput").ap()
        dbg_ktc = nc.dram_tensor("dbg_ktc", [128, 4, 256], BF16, kind="ExternalOutput").ap()
        dbg_vr = nc.dram_tensor("dbg_vr", [128, 16, 4, 2, 128], BF16, kind="ExternalOutput").ap()
        dbg_vc = nc.dram_tensor("dbg_vc", [128, 2, 4, 2, 128], BF16, kind="ExternalOutput").ap()

    b_x1 = [Buf("x1_%d" % i) for i in range(32)]
    b_x1b = [Buf("x1b_%d" % i) for i in range(32)]

    with ExitStack() as top:
        fw = FW(nc, top)

        def sb(stack, name, shape, dtype):
            return stack.enter_context(nc.sbuf_tensor("sb_" + name, list(shape), dtype)), Buf(name)

        def ps(name, shape, dtype):
            return top.enter_context(nc.psum_tensor("ps_" + name, list(shape), dtype)), Buf(name)

        tps = Rot([ps("tp%d" % i, [128, 1024], BF16) for i in range(1)])
        Fb, b_Fb = ps("Fb", [128, 512], F32)
        Gt = [ps("G%d" % i, [128, 1024], F32)[0] for i in range(2)]
        gb = [Buf("gbank%d" % i) for i in range(4)]
        banks = [(Gt[i // 2][:, (i % 2) * 512:(i % 2 + 1) * 512], gb[i]) for i in range(4)]
        pjs = Rot(banks)
        Ss = Rot(banks)
        S2s = Rot([(Gt[0], [gb[0], gb[1]]), (Gt[1], [gb[2], gb[3]])])
        Os = Rot([ps("O%d" % i, [128, 512], F32) for i in range(2)])

        ident, b_ident = sb(top, "ident", [128, 128], BF16)
        blk1, b_blk1 = sb(top, "blk1", [128, 128], BF16)
        RTm, b_RT = sb(top, "RTm", [128, 128], BF16)
        ones_bf, b_ones = sb(top, "ones_bf", [128, 128], BF16)
        eps_t, b_eps = sb(top, "eps_t", [128, 2], F32)
        AB, b_AB = sb(top, "AB", [128, 2, 4, 8], F32)
        gB, b_gB = sb(top, "gB", [128, 3, D], F32)
        ctx1, b_ctx1 = sb(top, "ctx1", [128, 2, D], F32)
        Gqk, b_Gqk = sb(top, "Gqk", [128, 2], F32)
        xins = Rot([sb(top, "xin%d" % i, [128, D], F32) + (fw.new_sem("d_xin%d" % i),) for i in range(2)])
        xress = Rot([sb(top, "xres%d" % i, [128, D], F32) + (fw.new_sem("d_xres%d" % i),) for i in range(2)])
        xns = Rot([sb(top, "xn%d" % i, [128, D], BF16) for i in range(2)])
        sts = Rot([sb(top, "st%d" % i, [128, 2], F32) for i in range(2)])
        hxT, b_hxT = sb(top, "hxT", [128, 8, 512], BF16)
        PT2l = [sb(top, "PT%d" % i, [128, 1024], BF16) for i in range(3)]
        PT2s = Rot(PT2l)
        PTs = Rot([(t[:, 0:512], b) for t, b in PT2l] + [(t[:, 512:1024], b) for t, b in PT2l[:1]])
        rden, b_rden = sb(top, "rden", [128, 512], F32)
        otmp, b_otmp = sb(top, "otmp", [128, 512], F32)
        s_w = [fw.new_sem("d_w%d" % i) for i in range(6)]
        s_c = [fw.new_sem("d_c%d" % i) for i in range(8)]
        s_out = fw.new_sem("d_out")
        s_bm = fw.new_sem("d_bm")

        fw.dma("pool", s_c[0], ident[:], ident_d[:, :], writes=[b_ident])
        fw.dma("pool", s_c[1], blk1[:], blk1_d[:, :], writes=[b_blk1])
        fw.dma("pool", s_c[2], RTm[:], RT_d[:, :], writes=[b_RT])
        fw.dma("sp", s_c[4], Gqk[:], qkg_d[:, :], writes=[b_Gqk])
        fw.op("dve", lambda e: e.memset(ones_bf[:], 1.0), writes=[b_ones])
        fw.op("dve", lambda e: e.memset(eps_t[:, 0:1], 1e-6), writes=[b_eps])
        fw.op("dve", lambda e: e.memset(eps_t[:, 1:2], 1.0), writes=[b_eps])
        fw.op("dve", lambda e: e.tensor_scalar(out=Gqk[:, 0:1], in0=Gqk[:, 0:1], scalar1=0.125, scalar2=None, op0=ALU.mult),
              reads=[b_Gqk], writes=[b_Gqk])

        l0w = ExitStack()
        w0, b_w0 = sb(l0w, "w0", [128, 8, 2560], BF16)
        wout0, b_wout0 = sb(l0w, "wout0", [128, 8, D], BF16)
        for c in range(8):
            fw.dma("pool", s_w[2 + c % 2], w0[:, c, :], w0_d[c * 128:(c + 1) * 128, :], writes=[b_w0])
        for c in range(8):
            fw.dma("pool", s_w[4 + c % 2], wout0[:, c, :], wout0_d[c * 128:(c + 1) * 128, :], writes=[b_wout0])

        with ExitStack() as p0:
            stgs = Rot([sb(p0, "wstg%d" % i, [128, 3072], F32) + (fw.new_sem("d_stg%d" % i),) for i in range(3)])
            cv, b_cv = sb(p0, "cv", [128, 8, 2], F32)
            svf, b_svf = sb(p0, "svf", [128, 8, 2], F32)
            svb, b_svb = sb(p0, "svb", [128, 2, 8, 128], F32)
            bmfm, b_bmfm = sb(p0, "bmfm", [128, 2, 24], F32)
            ngfm, b_ngfm = sb(p0, "ngfm", [128, 2, 8], F32)
            modfm, b_modfm = sb(p0, "modfm", [128, 24, 2], F32)
            bmB, b_bmB = sb(p0, "bmB", [128, D], F32)

            fw.dma("sp", s_c[5], cv[:], cvec_d[:, :, :], writes=[b_cv])
            fw.dma("sp", s_c[6], bmfm[:], bmfm_d[:, :, :], writes=[b_bmfm])
            fw.dma("sp", s_c[7], ngfm[:], ngfm_d[:, :, :], writes=[b_ngfm])
            fw.op("act", lambda e: e.activation(out=svf[:], in_=cv[:], func=AF.Silu), reads=[b_cv], writes=[b_svf])
            for w in range(2):
                for c in range(8):
                    fw.op("dve", lambda e, w=w, c=c: e.tensor_scalar(out=svb[:, w, c, :], in0=ones_bf[:], scalar1=svf[:, c, w:w + 1],
                                                                   scalar2=None, op0=ALU.mult),
                          reads=[b_ones, b_svf], writes=[b_svb])
            modrow, b_modrow = sb(p0, "modrow", [2, 3072], F32)
            ident32, b_id32 = sb(p0, "ident32", [128, 128], F32)
            s_id32 = fw.new_sem("d_id32")
            fw.dma("sp", s_id32, ident32[:], ident_d[:, :], writes=[b_id32])
            bmrow, b_bmrow = sb(p0, "bmrow", [2, 3072], F32)
            s_mr = fw.new_sem("d_modrow")
            for li in range(2):
                fw.dma("sp", s_bm, bmrow[:], bass.AP(bmod_d.tensor, li * 3072, [[0, 2], [1, 3072]]), writes=[b_bmrow])
                rowb = banks[0:4] + [Os.items[0], Os.items[1]]
                for c in range(8):
                    stg, b_stg, s_stg = stgs.next()
                    fw.dma("sp" if c % 2 == 0 else "act", s_stg, stg[:], wmod_d[li, c * 128:(c + 1) * 128, :], writes=[b_stg])
                    for nt in range(6):
                        rb_, b_rb = rowb[nt]
                        fw.op("pe", lambda e, nt=nt, c=c, stg=stg, rb_=rb_: e.matmul(
                            rb_[0:2, :], lhsT=svf[:, c, :], rhs=stg[:, nt * 512:(nt + 1) * 512],
                            start=(c == 0), stop=(c == 7)), reads=[b_stg, b_svf], writes=[b_rb], skip_self=True)
                for nt in range(6):
                    rb_, b_rb = rowb[nt]
                    fw.op("dve", lambda e, nt=nt, rb_=rb_: e.tensor_tensor(out=modrow[:, nt * 512:(nt + 1) * 512], in0=rb_[0:2, :],
                                                                         in1=bmrow[:, nt * 512:(nt + 1) * 512], op=ALU.add),
                          reads=[b_rb, b_bmrow], writes=[b_modrow], accum_w=True)
                fw.dma("sp", s_mr, modscr_d[li, :, :], modrow[:], reads=[b_modrow], writes=[b_modscr])
                tpm, b_tpm = banks[0]
                for a in range(24):
                    fw.op("pe", lambda e, a=a: e.transpose(tpm[:, 2 * a:2 * a + 2], modrow[0:2, a * 128:(a + 1) * 128], ident32[0:2, 0:2]),
                          reads=[b_modrow, b_id32], writes=[b_tpm], skip_self=True)
                fw.op("dve", lambda e: e.tensor_copy(out=modfm[:], in_=tpm[:, 0:48].rearrange("p (a b) -> p a b", b=2)),
                      reads=[b_tpm], writes=[b_modfm])
                for w in range(2):
                    fw.op("dve", lambda e, w=w: e.scalar_tensor_tensor(out=AB[:, li, 2 * w, :], in0=modfm[:, 8:16, w], scalar=1.0,
                                                                     in1=ngfm[:, li, :], op0=ALU.add, op1=ALU.mult),
                          reads=[b_modfm, b_ngfm], writes=[b_AB])
                    fw.op("dve", lambda e, w=w: e.tensor_copy(out=AB[:, li, 2 * w + 1, :], in_=modfm[:, 0:8, w]),
                          reads=[b_modfm], writes=[b_AB])
                for w in ([0, 1] if li == 0 else [0]):
                    gi = w if li == 0 else 2
                    fw.dma("sp", s_mr, gB[:, gi, :], bass.AP(modscr_d.tensor, (li * 2 + w) * 3072 + 2048, [[0, 128], [1, D]]),
                           reads=[b_modscr], writes=[b_gB])
            fw.barrier()
        if debug:
            fw.dma("sp", s_c[5], dbg_ab[:, :, :, :], AB[:], reads=[b_AB])
            fw.dma("sp", s_c[6], dbg_gb[:, :, :], gB[:], reads=[b_gB])

        evac_flip = [0]

        def norm_T(ntiles, li, which, src_dram=None, src_sbuf=None, src_bufs=None):
            for t in range(ntiles):
                norm_T_tile(t, li, which, src_dram, src_sbuf, src_bufs)

        def norm_T_tile(t, li, which, src_dram=None, src_sbuf=None, src_bufs=None, hx=None, b_hx=None):
            for _ in norm_T_tile_gen(t, li, which, src_dram, src_sbuf, src_bufs, hx, b_hx):
                pass

        def norm_T_tile_gen(t, li, which, src_dram=None, src_sbuf=None, src_bufs=None, hx=None, b_hx=None):
            hx_, b_hx_ = (hxT, b_hxT) if hx is None else (hx, b_hx)
            if True:
                if src_dram is not None:
                    xin, b_xin, s_xin = xins.next()
                    fw.dma("sp", s_xin, xin[:], src_dram[t * 128:(t + 1) * 128, :],
                           reads=([src_bufs[t]] if src_bufs is not None else []), writes=[b_xin])
                    xsrc, b_src = xin[:], b_xin
                else:
                    xsrc, b_src = src_sbuf[:, t, :], b_ctx1
                xn, b_xn = xns.next()
                st, b_st = sts.next()
                pass
                fw.op("act", lambda e: e.activation(out=xn[:], in_=xsrc, func=AF.Square, accum_out=st[:, 0:1]),
                      reads=[b_src, b_st], writes=[b_xn, b_st])
                fw.op("act", lambda e: e.activation(out=st[:, 1:2], in_=st[:, 0:1], func=AF.Ln, bias=eps_t[:, 0:1], scale=1.0 / D),
                      reads=[b_st, b_eps], writes=[b_st])
                fw.op("act", lambda e: e.activation(out=st[:, 1:2], in_=st[:, 1:2], func=AF.Exp, scale=-0.5),
                      reads=[b_st], writes=[b_st])
                fw.op("act", lambda e: e.activation(out=xn[:], in_=xsrc, func=AF.Identity, scale=st[:, 1:2]),
                      reads=[b_src, b_st], writes=[b_xn])
                yield
                tp, b_tp = tps.next()
                for c in range(8):
                    fw.op("pe", lambda e, c=c: e.transpose(tp[:, c * 128:(c + 1) * 128], xn[:, c * 128:(c + 1) * 128], ident[:]),
                          reads=[b_xn, b_ident], writes=[b_tp], skip_self=True)
                for c in range(8):
                    A_ap = AB[:, li, 2 * which, c:c + 1]
                    B_ap = AB[:, li, 2 * which + 1, c:c + 1]
                    dst = hx_[:, c, t * 128:(t + 1) * 128]
                    fw.op("dve", lambda e, c=c, dst=dst, A_ap=A_ap, B_ap=B_ap: e.tensor_scalar(
                        out=dst, in0=tp[:, c * 128:(c + 1) * 128], scalar1=A_ap, scalar2=B_ap, op0=ALU.mult, op1=ALU.add),
                        reads=[b_tp, b_AB], writes=[b_hx_], skip_self=True, accum_w=True)

        def silu_evac(pj, b_pj, ntok, dst, b_dst, scratch=None):
            sgt, b_sgt = (rden, b_rden) if scratch is None else scratch
            fw.op("act", lambda e: e.activation(out=sgt[:, 0:ntok], in_=pj[:, 0:ntok], func=AF.Exp, scale=-1.0),
                  reads=[b_pj], writes=[b_sgt])
            fw.op("act", lambda e: e.activation(out=sgt[:, 0:ntok], in_=sgt[:, 0:ntok], func=AF.Ln, bias=eps_t[:, 1:2], scale=1.0),
                  reads=[b_sgt, b_eps], writes=[b_sgt])
            fw.op("act", lambda e: e.activation(out=sgt[:, 0:ntok], in_=sgt[:, 0:ntok], func=AF.Exp, scale=-1.0),
                  reads=[b_sgt], writes=[b_sgt])
            fw.op("dve", lambda e: e.tensor_tensor(out=dst, in0=pj[:, 0:ntok], in1=sgt[:, 0:ntok], op=ALU.mult),
                  reads=[b_pj, b_sgt], writes=[b_dst], skip_self=False, accum_w=True)

        def proj_fm(wt, b_wt, col0, ntok, hx=None, b_hx=None, bank=None):
            hx_, b_hx_ = (hxT, b_hxT) if hx is None else (hx, b_hx)
            pj, b_pj = pjs.next() if bank is None else bank
            for c in range(8):
                fw.op("pe", lambda e, c=c: e.matmul(pj[:, 0:ntok], lhsT=wt[:, c, col0:col0 + 128], rhs=hx_[:, c, 0:ntok],
                                                  start=(c == 0), stop=(c == 7)),
                      reads=[b_wt, b_hx_], writes=[b_pj], skip_self=True)
            return pj, b_pj

        def out_proj_tile(ogT, b_ogT, npair, wout, b_wout, i, gi, xres=None, b_xres=None, ytmp=None, b_ytmp=None, bank=None):
            for nh in range(2):
                pj, b_pj = pjs.next() if bank is None else bank
                for pr in range(npair):
                    fw.op("pe", lambda e, pr=pr, pj=pj: e.matmul(pj[:, :], lhsT=ogT[:, pr, i * 128:(i + 1) * 128],
                                                               rhs=wout[:, pr, nh * 512:(nh + 1) * 512],
                                                               start=(pr == 0), stop=(pr == npair - 1)),
                          reads=[b_ogT, b_wout], writes=[b_pj], skip_self=True)
                if gi is None:
                    fw.op("dve", lambda e, pj=pj, nh=nh: e.tensor_tensor(out=xres[:, nh * 512:(nh + 1) * 512], in0=pj[:, :],
                                                                       in1=xres[:, nh * 512:(nh + 1) * 512], op=ALU.add),
                          reads=[b_pj, b_xres], writes=[b_xres])
                else:
                    fw.op("dve", lambda e, pj=pj, nh=nh: e.tensor_tensor(out=ytmp[:, nh * 512:(nh + 1) * 512], in0=pj[:, :],
                                                                       in1=gB[:, gi, nh * 512:(nh + 1) * 512], op=ALU.mult),
                          reads=[b_pj, b_gB], writes=[b_ytmp])

        def fold_gate(wout, b_wout, npair, gi):
            for pr in range(npair):
                fw.op("dve", lambda e, pr=pr: e.tensor_tensor(out=wout[:, pr, :], in0=wout[:, pr, :], in1=gB[:, gi, :], op=ALU.mult),
                      reads=[b_wout, b_gB], writes=[b_wout])

        with ExitStack() as l0:
            if stop_after == "p0":
                fw.barrier()
                return nc
            KT, b_KT = sb(l0, "KT", [128, 2, 4352], BF16)
            Vaug, b_V = sb(l0, "Vaug", [128, 34, 4, 128], BF16)
            cs, b_cs = sb(l0, "cs", [128, 2, 512], F32)
            QTp = [sb(l0, "QTp%d" % i, [128, 512], BF16) for i in range(2)]
            zTp = [sb(l0, "zTp%d" % i, [128, 512], BF16) for i in range(2)]
            ogTs = [sb(l0, "ogT%d" % i, [128, 8, 512], BF16) for i in range(2)]
            hxs = [(hxT, b_hxT), sb(l0, "hxT2", [128, 8, 512], BF16)]
            sq, b_sq = sb(l0, "sq", [128, 512], BF16)
            qh, b_qh = sb(l0, "qh", [128, 512], BF16)
            rs2, b_rs2 = sb(l0, "rs2", [128, 512], F32)
            t2, b_t2 = sb(l0, "t2", [128, 512], BF16)
            ocp, b_ocp = sb(l0, "ocp", [128, 512], F32)
            s_cs = fw.new_sem("d_cs")

            SK = ""
            if "M" not in SK:
                fw.op("pool", lambda e: e.memset(Vaug[:], 1.0), writes=[b_V])

            def load_cs(tok0, ntok):
                fw.dma("sp", s_cs, cs[:, 0, 0:ntok], ropeC_d[:, tok0:tok0 + ntok], writes=[b_cs])
                fw.dma("sp", s_cs, cs[:, 1, 0:ntok], ropeS_d[:, tok0:tok0 + ntok], writes=[b_cs])

            def qk_norm_rope(pj, b_pj, ntok, gcol, dst, b_dst, rope):
                fw.op("act", lambda e: e.activation(out=sq[:, 0:ntok], in_=pj[:, 0:ntok], func=AF.Square),
                      reads=[b_pj], writes=[b_sq])
                pj2, b_pj2 = pjs.next()
                fw.op("pe", lambda e: e.matmul(pj2[:, 0:ntok], lhsT=blk1[:], rhs=sq[:, 0:ntok], start=True, stop=True),
                      reads=[b_blk1, b_sq], writes=[b_pj2], skip_self=True)
                fw.op("act", lambda e: e.activation(out=rs2[:, 0:ntok], in_=pj2[:, 0:ntok], func=AF.Ln, bias=eps_t[:, 0:1], scale=1.0 / 64),
                      reads=[b_pj2, b_eps], writes=[b_rs2])
                fw.op("act", lambda e: e.activation(out=rs2[:, 0:ntok], in_=rs2[:, 0:ntok], func=AF.Exp, scale=-0.5),
                      reads=[b_rs2], writes=[b_rs2])
                tgt, b_tgt = (qh[:, 0:ntok], b_qh) if rope else (dst, b_dst)
                fw.op("dve", lambda e: e.scalar_tensor_tensor(out=tgt, in0=pj[:, 0:ntok], scalar=Gqk[:, gcol:gcol + 1],
                                                            in1=rs2[:, 0:ntok], op0=ALU.mult, op1=ALU.mult),
                      reads=[b_pj, b_rs2, b_Gqk], writes=[b_tgt])
                if rope:
                    pj3, b_pj3 = pjs.next()
                    fw.op("pe", lambda e: e.matmul(pj3[:, 0:ntok], lhsT=RTm[:], rhs=qh[:, 0:ntok], start=True, stop=True),
                          reads=[b_RT, b_qh], writes=[b_pj3], skip_self=True)
                    fw.op("pool", lambda e: e.tensor_tensor(out=dst, in0=qh[:, 0:ntok], in1=cs[:, 0, 0:ntok], op=ALU.mult),
                          reads=[b_qh, b_cs], writes=[b_dst])
                    fw.op("dve", lambda e: e.tensor_tensor(out=t2[:, 0:ntok], in0=pj3[:, 0:ntok], in1=cs[:, 1, 0:ntok], op=ALU.mult),
                          reads=[b_pj3, b_cs], writes=[b_t2])
                    fw.op("pool", lambda e: e.tensor_tensor(out=dst, in0=dst, in1=t2[:, 0:ntok], op=ALU.add),
                          reads=[b_dst, b_t2], writes=[b_dst])

            def l0_kv_units(ntok, koff, rope, tok0, hx, b_hx):
                if rope:
                    load_cs(tok0, ntok)
                for kvp in range(2):
                    pj, b_pj = proj_fm(w0, b_w0, 1024 + kvp * 128, ntok, hx, b_hx)
                    qk_norm_rope(pj, b_pj, ntok, 1, KT[:, kvp, koff:koff + ntok], b_KT, rope)
                    yield
                for i in range(ntok // 128):
                    pj, b_pj = pjs.next()
                    for c in range(8):
                        fw.op("pe", lambda e, c=c, pj=pj: e.matmul(pj[:, 0:256], lhsT=hx[:, c, i * 128:(i + 1) * 128],
                                                                 rhs=w0[:, c, 1280:1536], start=(c == 0), stop=(c == 7)),
                              reads=[b_hx, b_w0], writes=[b_pj], skip_self=True)
                    ch = koff // 128 + i
                    for kv in range(4):
                        e_ = kv % 2
                        dst = Vaug[:, ch, kv, 64 * e_:64 * e_ + 64]
                        fw.op("dve", lambda e, kv=kv, dst=dst, pj=pj: e.tensor_copy(out=dst, in_=pj[:, kv * 64:(kv + 1) * 64]),
                              reads=[b_pj], writes=[b_V], skip_self=True, accum_w=True)
                    yield

            def norm_group_gen(ntok, which, src, hx, b_hx):
                for t in range(ntok // 128):
                    for _ in norm_T_tile_gen(t, 0, which, src_dram=src, hx=hx, b_hx=b_hx):
                        yield
                    yield

            Pb = Os.items[1]
            Oacc, b_Oacc = Os.items[0]

            def q_norm_rope_1b(ntok, rope, hx, b_hx, pr, dst, b_dst):
                pj, b_pj = proj_fm(w0, b_w0, pr * 128, ntok, hx, b_hx, bank=Pb)
                fw.op("act", lambda e: e.activation(out=sq[:, 0:ntok], in_=pj[:, 0:ntok], func=AF.Square),
                      reads=[b_pj], writes=[b_sq])
                fw.op("dve", lambda e: e.tensor_scalar(out=qh[:, 0:ntok], in0=pj[:, 0:ntok], scalar1=Gqk[:, 0:1], scalar2=None,
                                                     op0=ALU.mult),
                      reads=[b_pj, b_Gqk, b_sq], writes=[b_qh])
                yield
                fw.op("pe", lambda e: e.matmul(pj[:, 0:ntok], lhsT=blk1[:], rhs=sq[:, 0:ntok], start=True, stop=True),
                      reads=[b_blk1, b_sq], writes=[b_pj], skip_self=True)
                fw.op("act", lambda e: e.activation(out=rs2[:, 0:ntok], in_=pj[:, 0:ntok], func=AF.Ln, bias=eps_t[:, 0:1], scale=1.0 / 64),
                      reads=[b_pj, b_eps], writes=[b_rs2])
                fw.op("act", lambda e: e.activation(out=rs2[:, 0:ntok], in_=rs2[:, 0:ntok], func=AF.Exp, scale=-0.5),
                      reads=[b_rs2], writes=[b_rs2])
                tgt, b_tgt = (qh[:, 0:ntok], b_qh) if rope else (dst, b_dst)
                fw.op("dve", lambda e: e.tensor_tensor(out=tgt, in0=qh[:, 0:ntok], in1=rs2[:, 0:ntok], op=ALU.mult),
                      reads=[b_qh, b_rs2], writes=[b_tgt])
                if rope:
                    yield
                    fw.op("pe", lambda e: e.matmul(pj[:, 0:ntok], lhsT=RTm[:], rhs=qh[:, 0:ntok], start=True, stop=True),
                          reads=[b_RT, b_qh], writes=[b_pj], skip_self=True)
                    fw.op("pool", lambda e: e.tensor_tensor(out=dst, in0=qh[:, 0:ntok], in1=cs[:, 0, 0:ntok], op=ALU.mult),
                          reads=[b_qh, b_cs], writes=[b_dst])
                    fw.op("dve", lambda e: e.tensor_tensor(out=t2[:, 0:ntok], in0=pj[:, 0:ntok], in1=cs[:, 1, 0:ntok], op=ALU.mult),
                          reads=[b_pj, b_cs], writes=[b_t2])
                    fw.op("pool", lambda e: e.tensor_tensor(out=dst, in0=dst, in1=t2[:, 0:ntok], op=ALU.add),
                          reads=[b_dst, b_t2], writes=[b_dst])

            def chain_gens(gens):
                for g_ in gens:
                    for _ in g_:
                        yield

            def one_tile(gen, nsub=2):
                for _ in range(nsub):
                    try:
                        next(gen)
                    except StopIteration:
                        return
                    yield

            def l0_prepG(g, hx, b_hx):
                for t in range(4):
                    for _ in norm_T_tile_gen(t, 0, 0, src_dram=x_d[g * 512:(g + 1) * 512, :], hx=hx, b_hx=b_hx):
                        yield
                    yield

            def l0_prepP(pr, ntok, rope, tok0, hx, b_hx, QTb, b_QTb, zTb, b_zTb, first_pair):
                if rope and first_pair:
                    load_cs(tok0, ntok)
                for _ in q_norm_rope_1b(ntok, rope, hx, b_hx, pr, QTb[:, 0:ntok], b_QTb):
                    yield
                yield
                pj, b_pj = proj_fm(w0, b_w0, 1536 + pr * 128, ntok, hx, b_hx, bank=Pb)
                silu_evac(pj, b_pj, ntok, zTb[:, 0:ntok], b_zTb, scratch=(rs2, b_rs2))
                yield

            def l0_epi(g, og, b_og):
                for i in range(4):
                    ti = g * 4 + i
                    xres, b_xres, s_xres = xress.next()
                    fw.dma("sp", s_xres, xres[:], x_d[ti * 128:(ti + 1) * 128, :], writes=[b_xres])
                    yield
                    out_proj_tile(og, b_og, 8, wout0, b_wout0, i, None, xres, b_xres, bank=Pb)
                    fw.dma("sp", s_xres, x1_d[ti * 128:(ti + 1) * 128, :], xres[:], reads=[b_xres], writes=[b_x1[ti]])
                    yield

            def l0_attnP(pr, ntok, chunks, QTb, b_QTb, zTb, b_zTb, og, b_og, bgq):
                npair = len(chunks) // 2
                step = 0
                for e_ in range(2):
                    kvp = pr // 4
                    kv = 2 * kvp + e_
                    me = slice(64 * e_, 64 * e_ + 64)
                    oth = slice(64 * (1 - e_), 64 * (1 - e_) + 64)
                    O, b_O = Oacc, b_Oacc
                    pend = None
                    for jp in range(npair):
                        S2, bS = S2s.next()
                        cs2 = (chunks[2 * jp], chunks[2 * jp + 1])
                        for k, s_ in enumerate(cs2):
                            fw.op("pe", lambda e, k=k, s_=s_, S2=S2: e.matmul(
                                S2[:, k * 512:k * 512 + ntok], lhsT=KT[me, kvp, s_ * 128:(s_ + 1) * 128],
                                rhs=QTb[me, 0:ntok], start=True, stop=True),
                                reads=[b_KT, b_QTb], writes=[bS[k]], skip_self=True)
                        PT2, b_PT2 = PT2s.next()
                        if ntok == 512:
                            i_ap, o_ap = S2[:, :], PT2[:, :]
                        else:
                            i_ap = S2[:, :].rearrange("p (a b) -> p a b", b=512)[:, :, 0:ntok]
                            o_ap = PT2[:, :].rearrange("p (a b) -> p a b", b=512)[:, :, 0:ntok]
                        fw.op("act", lambda e, i_ap=i_ap, o_ap=o_ap: e.activation(out=o_ap, in_=i_ap, func=AF.Exp),
                              reads=bS, writes=[b_PT2])
                        if pend is not None:
                            pend()
                        if NFILL and ntok == 512:
                            for _f in range(NFILLN):
                                fw.op("pe", lambda e: e.matmul(Fb[:, 0:NFILL], lhsT=ident[:], rhs=hxT[:, 0, 0:NFILL],
                                                             start=True, stop=True),
                                      reads=[b_ident], writes=[b_Fb], skip_self=True)

                        def pv(jp=jp, cs2=cs2, PT2=PT2, b_PT2=b_PT2):
                            for k, s_ in enumerate(cs2):
                                fw.op("pe", lambda e, k=k, s_=s_: e.matmul(
                                    O[:, 0:ntok], lhsT=Vaug[:, s_, kv, :], rhs=PT2[:, k * 512:k * 512 + ntok],
                                    start=(jp == 0 and k == 0), stop=(jp == npair - 1 and k == 1)),
                                    reads=[b_V, b_PT2], writes=[b_O], skip_self=True)
                        pend = pv
                        if bgq is not None and step % BG_EVERY == 1:
                            next(bgq, None)
                        step += 1
                    pend()
                    fw.op("dve", lambda e: e.tensor_copy(out=ocp[:, 0:ntok], in_=O[:, 0:ntok]), reads=[b_O], writes=[b_ocp])
                    fw.op("dve", lambda e: e.reciprocal(out=rden[me, 0:ntok], in_=ocp[oth, 0:ntok]), reads=[b_ocp], writes=[b_rden])
                    fw.op("dve", lambda e: e.tensor_tensor(out=otmp[me, 0:ntok], in0=ocp[me, 0:ntok], in1=rden[me, 0:ntok], op=ALU.mult),
                          reads=[b_ocp, b_rden], writes=[b_otmp])
                    fw.op("pool", lambda e: e.tensor_tensor(out=og[me, pr, 0:ntok], in0=otmp[me, 0:ntok], in1=zTb[me, 0:ntok],
                                                           op=ALU.mult),
                          reads=[b_otmp, b_zTb], writes=[b_og])

            kv_groups = [(LC, 0, False, 0, ctx_d, 1)] + [
                (512, LC + 512 * g, True, 512 * g, x_d[g * 512:(g + 1) * 512, :], 0)
                for g in range(0 if stop_after in ("l0n", "l0k", "l0v") else 8)]
            for _ in norm_group_gen(kv_groups[0][0], kv_groups[0][5], kv_groups[0][4], *hxs[0]):
                pass
            for gi_, (ntok_, koff_, rope_, tok0_, src_, which_) in enumerate(kv_groups):
                nxt = None
                if gi_ + 1 < len(kv_groups):
                    n2 = kv_groups[gi_ + 1]
                    nxt = norm_group_gen(n2[0], n2[5], n2[4], *hxs[(gi_ + 1) % 2])
                for _ in l0_kv_units(ntok_, koff_, rope_, tok0_, *hxs[gi_ % 2]):
                    if nxt is not None:
                        next(nxt, None)
                        next(nxt, None)
                if nxt is not None:
                    for _ in nxt:
                        pass
            if debug:
                fw.dma("sp", s_c[5], dbg_kt[:, :, :], KT[:], reads=[b_KT])
                fw.dma("sp", s_c[6], dbg_v[:, :, :, :], Vaug[:], reads=[b_V])
            if stop_after not in ("l0kv", "l0n", "l0k", "l0v"):
                ogc, b_ogc = ogTs[1]
                norm_T(2, 0, 1, src_dram=ctx_d)
                for pr in range(8):
                    QTb, b_QTb = QTp[pr % 2]
                    zTb, b_zTb = zTp[pr % 2]
                    import os
                    BX = os.environ.get("BX", "")
                    if BX == "skip":
                        continue
                    for _iu, _ in enumerate(l0_prepP(pr, LC, False, 0, hxT, b_hxT, QTb, b_QTb, zTb, b_zTb, False)):
                        if BX == "q" :
                            break
                    if BX in ("q", "qz"):
                        continue
                    l0_attnP(pr, LC, [0, 1], QTb, b_QTb, zTb, b_zTb, ogc, b_ogc, None)
                for i in range(2):
                    xres, b_xres, s_xres = xress.next()
                    ytmp, b_ytmp, _s = xins.next()
                    fw.dma("sp", s_xres, xres[:], ctx_d[i * 128:(i + 1) * 128, :], writes=[b_xres])
                    out_proj_tile(ogc, b_ogc, 8, wout0, b_wout0, i, 1, ytmp=ytmp, b_ytmp=b_ytmp)
                    fw.op("pool", lambda e, i=i, xres=xres, ytmp=ytmp: e.tensor_tensor(out=ctx1[:, i, :], in0=xres[:], in1=ytmp[:], op=ALU.add),
                          reads=[b_xres, b_ytmp], writes=[b_ctx1])
                fold_gate(wout0, b_wout0, 8, 0)
                if debug:
                    fw.dma("sp", s_c[7], dbg_ctx1[:, :, :], ctx1[:], reads=[b_ctx1])
                ng = 8 if stop_after is None or not stop_after.startswith("l0g") else int(stop_after[3:])
                import os
                if os.environ.get("FAST"):
                    ng = 0
                allch = list(range(34))
                if ng > 0:
                    for _ in l0_prepG(0, *hxs[0]):
                        pass
                    for _ in l0_prepP(0, 512, True, 0, *hxs[0], *QTp[0], *zTp[0], True):
                        pass
                k = 0
                epi = None
                for g in range(ng):
                    og, b_og = ogTs[g % 2]
                    genG = None
                    for pr in range(8):
                        gens = []
                        genP = None
                        if pr < 7:
                            genP = l0_prepP(pr + 1, 512, True, 512 * g, *hxs[g % 2], *QTp[(k + 1) % 2], *zTp[(k + 1) % 2], False)
                        elif g + 1 < ng:
                            genP = l0_prepP(0, 512, True, 512 * (g + 1), *hxs[(g + 1) % 2], *QTp[(k + 1) % 2], *zTp[(k + 1) % 2], True)
                        if genP is not None:
                            gens.append(genP)
                        if pr < 4 and epi is not None:
                            gens.append(one_tile(epi))
                        if pr == 3 and g + 1 < ng:
                            genG = l0_prepG(g + 1, *hxs[(g + 1) % 2])
                        if 3 <= pr <= 6 and genG is not None:
                            gens.append(one_tile(genG, 3))
                        bgq = chain_gens(gens)
                        QTb, b_QTb = QTp[k % 2]
                        zTb, b_zTb = zTp[k % 2]
                        l0_attnP(pr, 512, allch, QTb, b_QTb, zTb, b_zTb, og, b_og, bgq)
                        for _ in bgq:
                            pass
                        k += 1
                    if epi is not None:
                        for _ in epi:
                            pass
                    if genG is not None:
                        for _ in genG:
                            pass
                    epi = l0_epi(g, og, b_og)
                if epi is not None:
                    for _ in epi:
                        pass
            fw.barrier()
        l0w.close()

        if stop_after is None:
            with ExitStack() as l1:
                w1, b_w1 = sb(l1, "w1", [128, 8, 2048], BF16)
                wout1, b_wout1 = sb(l1, "wout1", [128, 4, D], BF16)
                tbs, b_tb = sb(l1, "tbs", [128, NM, 8, 128], BF16)
                KTr, b_KTr = sb(l1, "KTr", [128, 4, 16 * 128], BF16)
                KTc, b_KTc = sb(l1, "KTc", [128, 4, LC], BF16)
                Vr, b_Vr = sb(l1, "Vr", [128, 16, 4, 2, 128], BF16)
                Vc, b_Vc = sb(l1, "Vc", [128, 2, 4, 2, 128], BF16)
                QTs = [sb(l1, "QT1_%d" % i, [128, 4, 512], BF16) for i in range(3)]
                zTs = [sb(l1, "zT1_%d" % i, [128, 4, 512], BF16) for i in range(3)]
                og1s = [sb(l1, "og1_%d" % i, [128, 4, 512], BF16) for i in range(2)]
                s_m = fw.new_sem("d_rmask")
                s_tb = fw.new_sem("d_tb")

                fw.dma("sp", s_c[3], gB[:, 0, :], bass.AP(fg_d.tensor, 0, [[0, 128], [1, D]]), writes=[b_gB])
                fw.op("pool", lambda e: e.memset(Vr[:], 1.0), writes=[b_Vr])
                fw.op("pool", lambda e: e.memset(Vc[:], 1.0), writes=[b_Vc])

                def l1_v_evac(pj, b_pj, dstV, b_dst, ch):
                    pjv = pj[:, :].rearrange("p (a b c) -> p a b c", b=2, c=64)
                    fw.op("dve", lambda e: e.tensor_copy(out=dstV[:, ch, :, 0, 0:64], in_=pjv[:, :, 0, :]),
                          reads=[b_pj], writes=[b_dst], skip_self=True, accum_w=True)
                    fw.op("dve", lambda e: e.tensor_copy(out=dstV[:, ch, :, 1, 64:128], in_=pjv[:, :, 1, :]),
                          reads=[b_pj], writes=[b_dst], skip_self=True, accum_w=True)

                def l1_stageA(ntok, is_ctx, g, src_dram=None, src_sbuf=None, src_bufs=None):
                    for t in range(ntok // 128):
                        for _ in norm_T_tile_gen(t, 1, 1 if is_ctx else 0, src_dram, src_sbuf, src_bufs):
                            yield
                        yield
                    slot = 0 if is_ctx else g % 4
                    for pl in range(4):
                        pj, b_pj = proj_fm(w1, b_w1, 512 + pl * 128, ntok)
                        dst, b_dst = (KTc[:, pl, 0:ntok], b_KTc) if is_ctx else (KTr[:, pl, slot * 512:slot * 512 + ntok], b_KTr)
                        fw.op("dve", lambda e, pj=pj, dst=dst: e.tensor_copy(out=dst, in_=pj[:, 0:ntok]),
                              reads=[b_pj], writes=[b_dst], skip_self=True, accum_w=True)
                        yield
                    for i in range(ntok // 128):
                        pj, b_pj = pjs.next()
                        for c in range(8):
                            fw.op("pe", lambda e, c=c, pj=pj: e.matmul(pj[:, :], lhsT=hxT[:, c, i * 128:(i + 1) * 128],
                                                                     rhs=w1[:, c, 1024:1536], start=(c == 0), stop=(c == 7)),
                                  reads=[b_hxT, b_w1], writes=[b_pj], skip_self=True)
                        if is_ctx:
                            l1_v_evac(pj, b_pj, Vc, b_Vc, i)
                        else:
                            l1_v_evac(pj, b_pj, Vr, b_Vr, slot * 4 + i)
                        yield
                    if not is_ctx:
                        QT1, b_QT1 = QTs[g % 3]
                        zT1, b_zT1 = zTs[g % 3]
                        for pl in range(4):
                            pj, b_pj = proj_fm(w1, b_w1, pl * 128, ntok)
                            fw.op("dve", lambda e, pj=pj, pl=pl: e.tensor_scalar(out=QT1[:, pl, :], in0=pj[:, :], scalar1=0.125,
                                                                              scalar2=None, op0=ALU.mult),
                                  reads=[b_pj], writes=[b_QT1], skip_self=True, accum_w=True)
                            yield
                        for pl in range(4):
                            pj, b_pj = proj_fm(w1, b_w1, 1536 + pl * 128, ntok)
                            silu_evac(pj, b_pj, ntok, zT1[:, pl, :], b_zT1)
                            yield

                def drain(gen):
                    if gen is not None:
                        for _ in gen:
                            pass

                def l1_stageB(g, p, gen=None, epi=None):
                    og1, b_og1 = og1s[g % 2]
                    nblk = [0]
                    QT1, b_QT1 = QTs[g % 3]
                    zT1, b_zT1 = zTs[g % 3]
                    for qi in range(4):
                        qb = 4 * g + qi
                        qs = slice(qi * 128, (qi + 1) * 128)
                        klist = [(m, True) for m in chunks_of[qb]] + [(0, False), (1, False)]
                        n = len(klist)
                        for hq in range(2):
                            O, b_O = Os.next()
                            pendq = []
                            for j, (m, nb) in enumerate(klist):
                                S, b_S = Ss.next()
                                import os
                                NOB = os.environ.get("L1B", "")
                                if nb:
                                    di = m - qb + 3
                                    mi = mask_of[(qb, m)]
                                    ridx = ((m // 4) % 4) * 4 + (m % 4)
                                    fw.op("pe", lambda e, S=S, mi=mi: e.matmul(
                                        S[:, :], lhsT=ident[:], rhs=tbs[:, mi, 4 * hq:4 * hq + 4, :].rearrange("p a b -> p (a b)"),
                                        start=True, stop=False), reads=[b_ident, b_tb], writes=[b_S], skip_self=True)
                                for hi in range(4):
                                    e_ = hq
                                    pl = hi
                                    me = slice(64 * e_, 64 * e_ + 64)
                                    if nb:
                                        lhs = KTr[me, pl, ridx * 128:(ridx + 1) * 128]
                                        rb = b_KTr
                                    else:
                                        lhs = KTc[me, pl, m * 128:(m + 1) * 128]
                                        rb = b_KTc
                                    fw.op("pe", lambda e, S=S, hi=hi, lhs=lhs, me=me, pl=pl: e.matmul(
                                        S[:, hi * 128:(hi + 1) * 128], lhsT=lhs, rhs=QT1[me, pl, qs],
                                        start=((not nb) and hi == 0), stop=(hi == 3)), reads=[rb, b_QT1], writes=[b_S], skip_self=True)
                                PT, b_PT = PTs.next()
                                if NOB == "qkonly":
                                    continue
                                fw.op("act", lambda e, S=S, PT=PT: e.activation(out=PT[:, :], in_=S[:, :], func=AF.Exp),
                                      reads=[b_S], writes=[b_PT])
                                if NOB == "nopv":
                                    continue
                                if len(pendq) >= 2:
                                    pendq.pop(0)()
                                if j >= 1:
                                    nblk[0] += 1
                                    if epi is not None and nblk[0] % 3 == 0:
                                        next(epi, None)
                                    elif gen is not None:
                                        next(gen, None)

                                def pv(j=j, m=m, nb=nb, PT=PT, b_PT=b_PT, O=O, b_O=b_O):
                                    for hi in range(4):
                                        if nb:
                                            ridx = ((m // 4) % 4) * 4 + (m % 4)
                                            lhs = Vr[:, ridx, hi, hq, :]
                                            rb = b_Vr
                                        else:
                                            lhs = Vc[:, m, hi, hq, :]
                                            rb = b_Vc
                                        fw.op("pe", lambda e, hi=hi, lhs=lhs: e.matmul(
                                            O[:, hi * 128:(hi + 1) * 128], lhsT=lhs, rhs=PT[:, hi * 128:(hi + 1) * 128],
                                            start=(j == 0 and hi == 0), stop=(j == n - 1 and hi == 3)), reads=[rb, b_PT], writes=[b_O], skip_self=True)
                                pendq.append(pv)
                            for f_ in pendq:
                                f_()
                            import os
                            L1T = os.environ.get("L1T", "")
                            if L1T == "b1":
                                continue
                            Ov = O[:, :].rearrange("p (a c) -> p a c", c=128)
                            rdv = rden[:, :].rearrange("p (a c) -> p a c", c=128)
                            otv = otmp[:, :].rearrange("p (a c) -> p a c", c=128)
                            me = slice(64 * hq, 64 * hq + 64)
                            oth = slice(64 * (1 - hq), 64 * (1 - hq) + 64)
                            fw.op("dve", lambda e, me=me, oth=oth: e.reciprocal(out=rdv[me, :, :], in_=Ov[oth, :, :]),
                                  reads=[b_O], writes=[b_rden])
                            fw.op("dve", lambda e, me=me: e.tensor_tensor(out=otv[me, :, :], in0=Ov[me, :, :], in1=rdv[me, :, :], op=ALU.mult),
                                  reads=[b_O, b_rden], writes=[b_otmp])
                            fw.op("pool", lambda e, me=me: e.tensor_tensor(out=og1[me, :, qs], in0=otv[me, :, :], in1=zT1[me, :, qs], op=ALU.mult),
                                  reads=[b_otmp, b_zT1], writes=[b_og1])
                            nblk[0] += 1
                    if debug and p == 0 and g == 0:
                        fw.dma("sp", s_c[0], dbg_og1[:, :, :], og1[:], reads=[b_og1])
                        fw.dma("sp", s_c[1], dbg_qt1[:, :, :], QT1[:], reads=[b_QT1])
                        fw.dma("sp", s_c[2], dbg_zt1[:, :, :], zT1[:], reads=[b_zT1])
                        fw.dma("sp", s_c[3], dbg_ktr[:, :, :], KTr[:], reads=[b_KTr])
                        fw.dma("sp", s_c[4], dbg_ktc[:, :, :], KTc[:], reads=[b_KTc])
                        fw.dma("sp", s_c[5], dbg_vr[:, :, :, :, :], Vr[:], reads=[b_Vr])
                        fw.dma("sp", s_c[6], dbg_vc[:, :, :, :, :], Vc[:], reads=[b_Vc])

                def l1_epilogue(g, p):
                    og1, b_og1 = og1s[g % 2]
                    for i in range(4):
                        ti = g * 4 + i
                        xres, b_xres, s_xres = xress.next()
                        if p == 0:
                            fw.dma("sp", s_xres, xres[:], x1_d[ti * 128:(ti + 1) * 128, :], reads=[b_x1[ti]], writes=[b_xres])
                        else:
                            fw.dma("sp", s_xres, xres[:], x1b_d[ti * 128:(ti + 1) * 128, :], reads=[b_x1b[ti]], writes=[b_xres])
                        yield
                        out_proj_tile(og1, b_og1, 4, wout1, b_wout1, i, None, xres, b_xres)
                        if p == 0:
                            fw.dma("sp", s_xres, x1b_d[ti * 128:(ti + 1) * 128, :], xres[:], reads=[b_xres], writes=[b_x1b[ti]])
                        else:
                            xn, b_xn = xns.next()
                            st, b_st = sts.next()
                            yo, b_yo, s_yo = xins.next()
                            fw.op("act", lambda e, xn=xn, st=st, xres=xres: e.activation(out=xn[:], in_=xres[:], func=AF.Square,
                                                                                      accum_out=st[:, 0:1]),
                                  reads=[b_xres, b_st], writes=[b_xn, b_st])
                            fw.op("act", lambda e, st=st: e.activation(out=st[:, 1:2], in_=st[:, 0:1], func=AF.Ln, bias=eps_t[:, 0:1],
                                                                     scale=1.0 / D),
                                  reads=[b_st, b_eps], writes=[b_st])
                            fw.op("act", lambda e, st=st: e.activation(out=st[:, 1:2], in_=st[:, 1:2], func=AF.Exp, scale=-0.5),
                                  reads=[b_st], writes=[b_st])
                            fw.op("dve", lambda e, st=st, xres=xres, yo=yo: e.scalar_tensor_tensor(
                                out=yo[:], in0=xres[:], scalar=st[:, 1:2], in1=gB[:, 0, :], op0=ALU.mult, op1=ALU.mult),
                                reads=[b_xres, b_st, b_gB], writes=[b_yo])
                            fw.dma("sp", s_yo, out_d[ti * 128:(ti + 1) * 128, :], yo[:], reads=[b_yo])
                        yield

                for p in range(2):
                    for c in range(8):
                        fw.dma("pool", s_w[c % 2], w1[:, c, :], w1_d[p, c * 128:(c + 1) * 128, :], writes=[b_w1])
                    for pl in range(4):
                        r0_ = 512 * p + pl * 128
                        fw.dma("pool", s_w[2 + pl % 2], wout1[:, pl, :], wout1_d[r0_:r0_ + 128, :], writes=[b_wout1])
                    for di in range(NM):
                        fw.dma("pool", s_tb, tbs[:, di, :, :], tb_d[p, :, di, :, :], writes=[b_tb])
                    import os
                    L1T = os.environ.get("L1T", "")
                    drain(l1_stageA(LC, True, 0, src_sbuf=ctx1))
                    drain(l1_stageA(512, False, 0, src_dram=x1_d[0:512, :], src_bufs=b_x1[0:4]))
                    drain(l1_stageA(512, False, 1, src_dram=x1_d[512:1024, :], src_bufs=b_x1[4:8]))
                    fold_gate(wout1, b_wout1, 4, 2)
                    epi = None
                    for g in range(8):
                        gen = None
                        if g + 2 < 8:
                            gen = l1_stageA(512, False, g + 2, src_dram=x1_d[(g + 2) * 512:(g + 3) * 512, :],
                                            src_bufs=b_x1[(g + 2) * 4:(g + 3) * 4])
                        l1_stageB(g, p, gen, epi)
                        drain(gen)
                        drain(epi)
                        epi = l1_epilogue(g, p)
                    drain(epi)
                fw.barrier()
        fw.barrier()
    return nc


_CACHE = {}


def kernel(**inputs):
    per_core, meta = _prep(inputs)
    nc = build(meta)
    res = run_bass_kernel_spmd(nc, per_core, core_ids=list(range(len(per_core))))
    out = np.stack([np.asarray(r["out"], dtype=np.float32) for r in res.results], axis=0)
    return out
```

```python
import numpy as np
from contextlib import ExitStack
import concourse.bass as bass
import concourse.mybir as mybir
from concourse.bass_utils import run_bass_kernel_spmd

F32 = mybir.dt.float32
BF16 = mybir.dt.bfloat16
AF = mybir.ActivationFunctionType
ALU = mybir.AluOpType

T = 4096
D = 1024
LC = 256
NEG = -30000.0
HEAD_PERM = [0, 4, 1, 5, 2, 6, 3, 7, 8, 12, 9, 13, 10, 14, 11, 15]
SEM_ROLL = 12000
NFILL = 256
NFILLN = 1
BG_EVERY = 3


class Sem:
    def __init__(self, handle, name):
        self.h = handle
        self.name = name
        self.count = 0


class Buf:
    __slots__ = ("name", "writers", "readers", "base_writers")

    def __init__(self, name):
        self.name = name
        self.writers = {}
        self.base_writers = {}
        self.readers = {}


class EngQ:
    def __init__(self, fw, name, eng):
        self.fw = fw
        self.name = name
        self.eng = eng
        self.sem = fw.new_sem("e_" + name)
        self.waited = {}
        self.nroll = 0

    def wait(self, tok):
        if tok is None:
            return
        sem, val = tok
        if self.waited.get(sem, 0) >= val:
            return
        self.eng.wait_ge(sem.h, val)
        self.waited[sem] = val


class Rot:
    def __init__(self, items):
        self.items = items
        self.i = 0

    def next(self):
        it = self.items[self.i % len(self.items)]
        self.i += 1
        return it


class FW:
    def __init__(self, nc, stack):
        self.nc = nc
        self.stack = stack
        self.all_sems = []
        self.E = {}
        for name, eng in (("pe", nc.tensor), ("act", nc.scalar), ("dve", nc.vector),
                          ("pool", nc.gpsimd), ("sp", nc.sync)):
            self.E[name] = EngQ(self, name, eng)

    def new_sem(self, name):
        h = self.stack.enter_context(self.nc.semaphore(name))
        s = Sem(h, name)
        self.all_sems.append(s)
        return s

    def _deps(self, q, reads, writes, skip_self, accum_w=False):
        for b in reads:
            for s, v in b.writers.items():
                if skip_self and s is q.sem:
                    continue
                q.wait((s, v))
        for b in writes:
            for s, v in b.readers.items():
                if skip_self and s is q.sem:
                    continue
                q.wait((s, v))
            for s, v in (b.base_writers if accum_w else b.writers).items():
                if skip_self and s is q.sem:
                    continue
                q.wait((s, v))

    @staticmethod
    def _commit(tok, reads, writes, accum_w=False):
        s, v = tok
        for b in reads:
            if b.readers.get(s, 0) < v:
                b.readers[s] = v
        for b in writes:
            if b.writers.get(s, 0) < v:
                b.writers[s] = v
            if not accum_w and b.base_writers.get(s, 0) < v:
                b.base_writers[s] = v

    def op(self, ename, build, reads=(), writes=(), skip_self=False, accum_w=False):
        q = self.E[ename]
        if q.sem.count >= SEM_ROLL:
            q.nroll += 1
            q.sem = self.new_sem("e_%s_%d" % (ename, q.nroll))
        self._deps(q, reads, writes, skip_self, accum_w)
        ins = build(q.eng)
        q.sem.count += 1
        ins.then_inc(q.sem.h, 1)
        tok = (q.sem, q.sem.count)
        self._commit(tok, reads, writes, accum_w)
        return tok

    def dma(self, ename, sem, out, in_, reads=(), writes=()):
        q = self.E[ename]
        self._deps(q, reads, writes, False)
        ins = q.eng.dma_start(out=out, in_=in_)
        sem.count += 16
        ins.then_inc(sem.h, 16)
        tok = (sem, sem.count)
        self._commit(tok, reads, writes)
        return tok

    def barrier(self):
        toks = [(s, s.count) for s in self.all_sems if s.count > 0]
        for q in self.E.values():
            for t in toks:
                q.wait(t)


def _l1_window_meta():
    r0 = lambda r: int(np.clip(r - 4, 0, 56))
    masks = []
    mask_of = {}
    key_to_idx = {}
    chunks = {}
    for qb in range(32):
        lo = r0(2 * qb) // 2
        hi = (r0(2 * qb + 1) + 7) // 2
        chunks[qb] = list(range(lo, hi + 1))
        for m in chunks[qb]:
            mk = np.zeros((128, 128), np.float32)
            for kr2 in range(2):
                for qr2 in range(2):
                    kr = 2 * m + kr2
                    qr = 2 * qb + qr2
                    ok = r0(qr) <= kr < r0(qr) + 8
                    if not ok:
                        mk[kr2 * 64:(kr2 + 1) * 64, qr2 * 64:(qr2 + 1) * 64] = NEG
            key = mk.tobytes()
            if key not in key_to_idx:
                key_to_idx[key] = len(masks)
                masks.append(mk)
            mask_of[(qb, m)] = key_to_idx[key]
    return chunks, mask_of, np.stack(masks, 0)


def _bias_tiles(rpb):
    H = rpb.shape[0]
    kc = np.arange(64)[:, None]
    qc = np.arange(64)[None, :]
    c0 = np.clip(np.arange(64) - 8, 0, 48)[None, :]
    colok = (kc >= c0) & (kc < c0 + 16)
    dc = np.clip(kc - qc + 15, 0, 30)
    tb = np.zeros((H, 7, 128, 128), np.float32)
    for di in range(7):
        d = di - 3
        for kr2 in range(2):
            for qr2 in range(2):
                a = 2 * d + kr2 - qr2 + 7
                if 0 <= a < 15:
                    blk = rpb[:, a][:, dc]
                else:
                    blk = np.zeros((H, 64, 64), np.float32)
                blk = np.where(colok[None], blk, np.float32(NEG))
                tb[:, di, kr2 * 64:(kr2 + 1) * 64, qr2 * 64:(qr2 + 1) * 64] = blk
    return tb


def _rope_tables():
    pos = np.arange(T)
    row = (pos // 64).astype(np.float32)
    col = (pos % 64).astype(np.float32)
    inv = (np.float32(10000.0) ** (-np.arange(0, 32, 2, dtype=np.float32) / np.float32(32))).astype(np.float32)
    C = np.zeros((128, T), np.float32)
    S = np.zeros((128, T), np.float32)
    RT = np.zeros((128, 128), np.float32)
    for p in range(128):
        d = p % 64
        axis = d // 32
        half = (d % 32) // 16
        f = d % 16
        ang = (row if axis == 0 else col) * inv[f]
        C[p] = np.cos(ang)
        S[p] = np.sin(ang) * (-1.0 if half == 0 else 1.0)
        partner = p + 16 if half == 0 else p - 16
        RT[partner, p] = 1.0
    return C, S, RT


def _prep(inputs):
    f = lambda k: np.ascontiguousarray(np.asarray(inputs[k], dtype=np.float32))
    sh = {}
    sh["w_mod"] = f("w_mod")
    bm = f("b_mod")
    sh["bmod_fm"] = np.ascontiguousarray(bm.reshape(2, 24, 128).transpose(2, 0, 1))
    sh["bmod_g"] = np.ascontiguousarray(bm[:, 2048:3072])
    sh["b_mod"] = bm
    sh["norm_g_fm"] = np.ascontiguousarray(f("norm_g").reshape(2, 8, 128).transpose(2, 0, 1))
    sh["fg"] = f("final_norm_g").reshape(1, D)
    hcols = np.concatenate([np.arange(64) + 64 * h for h in HEAD_PERM])
    a_in = f("a_w_in")[0]
    sh["w0"] = np.ascontiguousarray(np.concatenate(
        [a_in[:, :1024][:, hcols], a_in[:, 1024:1536], a_in[:, 1536:][:, hcols]], axis=1))
    sh["wout0"] = np.ascontiguousarray(f("a_w_out")[0][hcols, :])
    qg = f("a_q_norm_g")[0]
    kg = f("a_k_norm_g")[0]
    sh["qkg"] = np.ascontiguousarray(np.stack([np.tile(qg, 2), np.tile(kg, 2)], axis=1))
    C, S, RT = _rope_tables()
    sh["ropeC"] = C
    sh["ropeS"] = S
    sh["RT"] = RT
    sh["ident"] = np.eye(128, dtype=np.float32)
    blk = np.zeros((128, 128), np.float32)
    blk[:64, :64] = 1.0
    blk[64:, 64:] = 1.0
    sh["blk1"] = blk
    b_in = f("b_w_in")[0]
    w1 = np.zeros((2, D, 2048), np.float32)
    for p in range(2):
        for part in range(4):
            w1[p, :, part * 512:(part + 1) * 512] = b_in[:, part * 1024 + p * 512: part * 1024 + (p + 1) * 512]
    sh["w1"] = w1
    sh["wout1"] = f("b_w_out")[0]
    tb = _bias_tiles(f("b_rpb")[0])
    chunks, mask_of, masks = _l1_window_meta()
    combos = sorted({(m - qb + 3, mask_of[(qb, m)]) for qb in range(32) for m in chunks[qb]})
    combo_of = {c: i for i, c in enumerate(combos)}
    tbm = np.stack([np.where(masks[mi][None] < 0, np.float32(NEG), tb[:, di]) for di, mi in combos], axis=1)
    nm = len(combos)
    sh["tb"] = np.ascontiguousarray(tbm.reshape(2, 8, nm, 128, 128)[:, [0, 2, 4, 6, 1, 3, 5, 7]].transpose(0, 3, 2, 1, 4))
    mask_of = {k: combo_of[(k[1] - k[0] + 3, v)] for k, v in mask_of.items()}
    cc = f("c_ctx").reshape(8, 128).T
    per_core = []
    x = np.asarray(inputs["x"], dtype=np.float32)
    ctx = np.asarray(inputs["ctx"], dtype=np.float32)
    c = np.asarray(inputs["c"], dtype=np.float32)
    for b in range(x.shape[0]):
        cv = np.stack([c[b].reshape(8, 128).T, cc], axis=2)
        m = dict(sh)
        m["x"] = np.ascontiguousarray(x[b])
        m["ctx"] = np.ascontiguousarray(ctx[b])
        m["cvec"] = np.ascontiguousarray(cv)
        per_core.append(m)
    return per_core, (chunks, mask_of, nm)


def build(meta, debug=False, stop_after=None):
    chunks_of, mask_of, NM = meta
    nc = bass.Bass("TRN2", target_bir_lowering=False)

    def din(name, shape):
        return nc.dram_tensor(name, list(shape), F32, kind="ExternalInput").ap()

    x_d = din("x", [T, D])
    ctx_d = din("ctx", [LC, D])
    cvec_d = din("cvec", [128, 8, 2])
    wmod_d = din("w_mod", [2, D, 3072])
    bmfm_d = din("bmod_fm", [128, 2, 24])
    bmg_d = din("bmod_g", [2, D])
    bmod_d = din("b_mod", [2, 3072])
    modscr_d = nc.dram_tensor("modscr", [2, 2, 3072], F32).ap()
    b_modscr = Buf("modscr")
    ngfm_d = din("norm_g_fm", [128, 2, 8])
    fg_d = din("fg", [1, D])
    w0_d = din("w0", [D, 2560])
    wout0_d = din("wout0", [D, D])
    qkg_d = din("qkg", [128, 2])
    ropeC_d = din("ropeC", [128, T])
    ropeS_d = din("ropeS", [128, T])
    RT_d = din("RT", [128, 128])
    ident_d = din("ident", [128, 128])
    blk1_d = din("blk1", [128, 128])
    w1_d = din("w1", [2, D, 2048])
    wout1_d = din("wout1", [D, D])
    tb_d = din("tb", [2, 128, NM, 8, 128])
    out_d = nc.dram_tensor("out", [T, D], F32, kind="ExternalOutput").ap()
    scratch_kind = "ExternalOutput" if debug else "Internal"
    x1_d = nc.dram_tensor("x1", [T, D], F32, kind=scratch_kind).ap()
    x1b_d = nc.dram_tensor("x1b", [T, D], F32, kind=scratch_kind).ap()
    if debug:
        dbg_ctx1 = nc.dram_tensor("dbg_ctx1", [128, 2, D], F32, kind="ExternalOutput").ap()
        dbg_kt = nc.dram_tensor("dbg_kt", [128, 2, 4352], BF16, kind="ExternalOutput").ap()
        dbg_v = nc.dram_tensor("dbg_v", [128, 34, 4, 128], BF16, kind="ExternalOutput").ap()
        dbg_ab = nc.dram_tensor("dbg_ab", [128, 2, 4, 8], F32, kind="ExternalOutput").ap()
        dbg_gb = nc.dram_tensor("dbg_gb", [128, 3, D], F32, kind="ExternalOutput").ap()
        dbg_og1 = nc.dram_tensor("dbg_og1", [128, 4, 512], BF16, kind="ExternalOutput").ap()
        dbg_qt1 = nc.dram_tensor("dbg_qt1", [128, 4, 512], BF16, kind="ExternalOutput").ap()
        dbg_zt1 = nc.dram_tensor("dbg_zt1", [128, 4, 512], BF16, kind="ExternalOutput").ap()
        dbg_ktr = nc.dram_tensor("dbg_ktr", [128, 4, 2048], BF16, kind="ExternalOutput").ap()
        dbg_ktc = nc.dram_tensor("dbg_ktc", [128, 4, 256], BF16, kind="ExternalOutput").ap()
        dbg_vr = nc.dram_tensor("dbg_vr", [128, 16, 4, 2, 128], BF16, kind="ExternalOutput").ap()
        dbg_vc = nc.dram_tensor("dbg_vc", [128, 2, 4, 2, 128], BF16, kind="ExternalOutput").ap()

    b_x1 = [Buf("x1_%d" % i) for i in range(32)]
    b_x1b = [Buf("x1b_%d" % i) for i in range(32)]

    with ExitStack() as top:
        fw = FW(nc, top)

        def sb(stack, name, shape, dtype):
            return stack.enter_context(nc.sbuf_tensor("sb_" + name, list(shape), dtype)), Buf(name)

        def ps(name, shape, dtype):
            return top.enter_context(nc.psum_tensor("ps_" + name, list(shape), dtype)), Buf(name)

        tps = Rot([ps("tp%d" % i, [128, 1024], BF16) for i in range(1)])
        Fb, b_Fb = ps("Fb", [128, 512], F32)
        Gt = [ps("G%d" % i, [128, 1024], F32)[0] for i in range(2)]
        gb = [Buf("gbank%d" % i) for i in range(4)]
        banks = [(Gt[i // 2][:, (i % 2) * 512:(i % 2 + 1) * 512], gb[i]) for i in range(4)]
        pjs = Rot(banks)
        Ss = Rot(banks)
        S2s = Rot([(Gt[0], [gb[0], gb[1]]), (Gt[1], [gb[2], gb[3]])])
        Os = Rot([ps("O%d" % i, [128, 512], F32) for i in range(2)])

        ident, b_ident = sb(top, "ident", [128, 128], BF16)
        blk1, b_blk1 = sb(top, "blk1", [128, 128], BF16)
        RTm, b_RT = sb(top, "RTm", [128, 128], BF16)
        ones_bf, b_ones = sb(top, "ones_bf", [128, 128], BF16)
        eps_t, b_eps = sb(top, "eps_t", [128, 2], F32)
        AB, b_AB = sb(top, "AB", [128, 2, 4, 8], F32)
        gB, b_gB = sb(top, "gB", [128, 3, D], F32)
        ctx1, b_ctx1 = sb(top, "ctx1", [128, 2, D], F32)
        Gqk, b_Gqk = sb(top, "Gqk", [128, 2], F32)
        xins = Rot([sb(top, "xin%d" % i, [128, D], F32) + (fw.new_sem("d_xin%d" % i),) for i in range(2)])
        xress = Rot([sb(top, "xres%d" % i, [128, D], F32) + (fw.new_sem("d_xres%d" % i),) for i in range(2)])
        xns = Rot([sb(top, "xn%d" % i, [128, D], BF16) for i in range(2)])
        sts = Rot([sb(top, "st%d" % i, [128, 2], F32) for i in range(2)])
        hxT, b_hxT = sb(top, "hxT", [128, 8, 512], BF16)
        PT2l = [sb(top, "PT%d" % i, [128, 1024], BF16) for i in range(3)]
        PT2s = Rot(PT2l)
        PTs = Rot([(t[:, 0:512], b) for t, b in PT2l] + [(t[:, 512:1024], b) for t, b in PT2l[:1]])
        rden, b_rden = sb(top, "rden", [128, 512], F32)
        otmp, b_otmp = sb(top, "otmp", [128, 512], F32)
        s_w = [fw.new_sem("d_w%d" % i) for i in range(6)]
        s_c = [fw.new_sem("d_c%d" % i) for i in range(8)]
        s_out = fw.new_sem("d_out")
        s_bm = fw.new_sem("d_bm")

        fw.dma("pool", s_c[0], ident[:], ident_d[:, :], writes=[b_ident])
        fw.dma("pool", s_c[1], blk1[:], blk1_d[:, :], writes=[b_blk1])
        fw.dma("pool", s_c[2], RTm[:], RT_d[:, :], writes=[b_RT])
        fw.dma("sp", s_c[4], Gqk[:], qkg_d[:, :], writes=[b_Gqk])
        fw.op("dve", lambda e: e.memset(ones_bf[:], 1.0), writes=[b_ones])
        fw.op("dve", lambda e: e.memset(eps_t[:, 0:1], 1e-6), writes=[b_eps])
        fw.op("dve", lambda e: e.memset(eps_t[:, 1:2], 1.0), writes=[b_eps])
        fw.op("dve", lambda e: e.tensor_scalar(out=Gqk[:, 0:1], in0=Gqk[:, 0:1], scalar1=0.125, scalar2=None, op0=ALU.mult),
              reads=[b_Gqk], writes=[b_Gqk])

        l0w = ExitStack()
        w0, b_w0 = sb(l0w, "w0", [128, 8, 2560], BF16)
        wout0, b_wout0 = sb(l0w, "wout0", [128, 8, D], BF16)
        for c in range(8):
            fw.dma("pool", s_w[2 + c % 2], w0[:, c, :], w0_d[c * 128:(c + 1) * 128, :], writes=[b_w0])
        for c in range(8):
            fw.dma("pool", s_w[4 + c % 2], wout0[:, c, :], wout0_d[c * 128:(c + 1) * 128, :], writes=[b_wout0])

        with ExitStack() as p0:
            stgs = Rot([sb(p0, "wstg%d" % i, [128, 3072], F32) + (fw.new_sem("d_stg%d" % i),) for i in range(3)])
            cv, b_cv = sb(p0, "cv", [128, 8, 2], F32)
            svf, b_svf = sb(p0, "svf", [128, 8, 2], F32)
            svb, b_svb = sb(p0, "svb", [128, 2, 8, 128], F32)
            bmfm, b_bmfm = sb(p0, "bmfm", [128, 2, 24], F32)
            ngfm, b_ngfm = sb(p0, "ngfm", [128, 2, 8], F32)
            modfm, b_modfm = sb(p0, "modfm", [128, 24, 2], F32)
            bmB, b_bmB = sb(p0, "bmB", [128, D], F32)

            fw.dma("sp", s_c[5], cv[:], cvec_d[:, :, :], writes=[b_cv])
            fw.dma("sp", s_c[6], bmfm[:], bmfm_d[:, :, :], writes=[b_bmfm])
            fw.dma("sp", s_c[7], ngfm[:], ngfm_d[:, :, :], writes=[b_ngfm])
            fw.op("act", lambda e: e.activation(out=svf[:], in_=cv[:], func=AF.Silu), reads=[b_cv], writes=[b_svf])
            for w in range(2):
                for c in range(8):
                    fw.op("dve", lambda e, w=w, c=c: e.tensor_scalar(out=svb[:, w, c, :], in0=ones_bf[:], scalar1=svf[:, c, w:w + 1],
                                                                   scalar2=None, op0=ALU.mult),
                          reads=[b_ones, b_svf], writes=[b_svb])
            modrow, b_modrow = sb(p0, "modrow", [2, 3072], F32)
            ident32, b_id32 = sb(p0, "ident32", [128, 128], F32)
            s_id32 = fw.new_sem("d_id32")
            fw.dma("sp", s_id32, ident32[:], ident_d[:, :], writes=[b_id32])
            bmrow, b_bmrow = sb(p0, "bmrow", [2, 3072], F32)
            s_mr = fw.new_sem("d_modrow")
            for li in range(2):
                fw.dma("sp", s_bm, bmrow[:], bass.AP(bmod_d.tensor, li * 3072, [[0, 2], [1, 3072]]), writes=[b_bmrow])
                rowb = banks[0:4] + [Os.items[0], Os.items[1]]
                for c in range(8):
                    stg, b_stg, s_stg = stgs.next()
                    fw.dma("sp" if c % 2 == 0 else "act", s_stg, stg[:], wmod_d[li, c * 128:(c + 1) * 128, :], writes=[b_stg])
                    for nt in range(6):
                        rb_, b_rb = rowb[nt]
                        fw.op("pe", lambda e, nt=nt, c=c, stg=stg, rb_=rb_: e.matmul(
                            rb_[0:2, :], lhsT=svf[:, c, :], rhs=stg[:, nt * 512:(nt + 1) * 512],
                            start=(c == 0), stop=(c == 7)), reads=[b_stg, b_svf], writes=[b_rb], skip_self=True)
                for nt in range(6):
                    rb_, b_rb = rowb[nt]
                    fw.op("dve", lambda e, nt=nt, rb_=rb_: e.tensor_tensor(out=modrow[:, nt * 512:(nt + 1) * 512], in0=rb_[0:2, :],
                                                                         in1=bmrow[:, nt * 512:(nt + 1) * 512], op=ALU.add),
                          reads=[b_rb, b_bmrow], writes=[b_modrow], accum_w=True)
                fw.dma("sp", s_mr, modscr_d[li, :, :], modrow[:], reads=[b_modrow], writes=[b_modscr])
                tpm, b_tpm = banks[0]
                for a in range(24):
                    fw.op("pe", lambda e, a=a: e.transpose(tpm[:, 2 * a:2 * a + 2], modrow[0:2, a * 128:(a + 1) * 128], ident32[0:2, 0:2]),
                          reads=[b_modrow, b_id32], writes=[b_tpm], skip_self=True)
                fw.op("dve", lambda e: e.tensor_copy(out=modfm[:], in_=tpm[:, 0:48].rearrange("p (a b) -> p a b", b=2)),
                      reads=[b_tpm], writes=[b_modfm])
                for w in range(2):
                    fw.op("dve", lambda e, w=w: e.scalar_tensor_tensor(out=AB[:, li, 2 * w, :], in0=modfm[:, 8:16, w], scalar=1.0,
                                                                     in1=ngfm[:, li, :], op0=ALU.add, op1=ALU.mult),
                          reads=[b_modfm, b_ngfm], writes=[b_AB])
                    fw.op("dve", lambda e, w=w: e.tensor_copy(out=AB[:, li, 2 * w + 1, :], in_=modfm[:, 0:8, w]),
                          reads=[b_modfm], writes=[b_AB])
                for w in ([0, 1] if li == 0 else [0]):
                    gi = w if li == 0 else 2
                    fw.dma("sp", s_mr, gB[:, gi, :], bass.AP(modscr_d.tensor, (li * 2 + w) * 3072 + 2048, [[0, 128], [1, D]]),
                           reads=[b_modscr], writes=[b_gB])
            fw.barrier()
        if debug:
            fw.dma("sp", s_c[5], dbg_ab[:, :, :, :], AB[:], reads=[b_AB])
            fw.dma("sp", s_c[6], dbg_gb[:, :, :], gB[:], reads=[b_gB])

        evac_flip = [0]

        def norm_T(ntiles, li, which, src_dram=None, src_sbuf=None, src_bufs=None):
            for t in range(ntiles):
                norm_T_tile(t, li, which, src_dram, src_sbuf, src_bufs)

        def norm_T_tile(t, li, which, src_dram=None, src_sbuf=None, src_bufs=None, hx=None, b_hx=None):
            for _ in norm_T_tile_gen(t, li, which, src_dram, src_sbuf, src_bufs, hx, b_hx):
                pass

        def norm_T_tile_gen(t, li, which, src_dram=None, src_sbuf=None, src_bufs=None, hx=None, b_hx=None):
            hx_, b_hx_ = (hxT, b_hxT) if hx is None else (hx, b_hx)
            if True:
                if src_dram is not None:
                    xin, b_xin, s_xin = xins.next()
                    fw.dma("sp", s_xin, xin[:], src_dram[t * 128:(t + 1) * 128, :],
                           reads=([src_bufs[t]] if src_bufs is not None else []), writes=[b_xin])
                    xsrc, b_src = xin[:], b_xin
                else:
                    xsrc, b_src = src_sbuf[:, t, :], b_ctx1
                xn, b_xn = xns.next()
                st, b_st = sts.next()
                pass
                fw.op("act", lambda e: e.activation(out=xn[:], in_=xsrc, func=AF.Square, accum_out=st[:, 0:1]),
                      reads=[b_src, b_st], writes=[b_xn, b_st])
                fw.op("act", lambda e: e.activation(out=st[:, 1:2], in_=st[:, 0:1], func=AF.Ln, bias=eps_t[:, 0:1], scale=1.0 / D),
                      reads=[b_st, b_eps], writes=[b_st])
                fw.op("act", lambda e: e.activation(out=st[:, 1:2], in_=st[:, 1:2], func=AF.Exp, scale=-0.5),
                      reads=[b_st], writes=[b_st])
                fw.op("act", lambda e: e.activation(out=xn[:], in_=xsrc, func=AF.Identity, scale=st[:, 1:2]),
                      reads=[b_src, b_st], writes=[b_xn])
                yield
                tp, b_tp = tps.next()
                for c in range(8):
                    fw.op("pe", lambda e, c=c: e.transpose(tp[:, c * 128:(c + 1) * 128], xn[:, c * 128:(c + 1) * 128], ident[:]),
                          reads=[b_xn, b_ident], writes=[b_tp], skip_self=True)
                for c in range(8):
                    A_ap = AB[:, li, 2 * which, c:c + 1]
                    B_ap = AB[:, li, 2 * which + 1, c:c + 1]
                    dst = hx_[:, c, t * 128:(t + 1) * 128]
                    fw.op("dve", lambda e, c=c, dst=dst, A_ap=A_ap, B_ap=B_ap: e.tensor_scalar(
                        out=dst, in0=tp[:, c * 128:(c + 1) * 128], scalar1=A_ap, scalar2=B_ap, op0=ALU.mult, op1=ALU.add),
                        reads=[b_tp, b_AB], writes=[b_hx_], skip_self=True, accum_w=True)

        def silu_evac(pj, b_pj, ntok, dst, b_dst, scratch=None):
            sgt, b_sgt = (rden, b_rden) if scratch is None else scratch
            fw.op("act", lambda e: e.activation(out=sgt[:, 0:ntok], in_=pj[:, 0:ntok], func=AF.Exp, scale=-1.0),
                  reads=[b_pj], writes=[b_sgt])
            fw.op("act", lambda e: e.activation(out=sgt[:, 0:ntok], in_=sgt[:, 0:ntok], func=AF.Ln, bias=eps_t[:, 1:2], scale=1.0),
                  reads=[b_sgt, b_eps], writes=[b_sgt])
            fw.op("act", lambda e: e.activation(out=sgt[:, 0:ntok], in_=sgt[:, 0:ntok], func=AF.Exp, scale=-1.0),
                  reads=[b_sgt], writes=[b_sgt])
            fw.op("dve", lambda e: e.tensor_tensor(out=dst, in0=pj[:, 0:ntok], in1=sgt[:, 0:ntok], op=ALU.mult),
                  reads=[b_pj, b_sgt], writes=[b_dst], skip_self=False, accum_w=True)

        def proj_fm(wt, b_wt, col0, ntok, hx=None, b_hx=None, bank=None):
            hx_, b_hx_ = (hxT, b_hxT) if hx is None else (hx, b_hx)
            pj, b_pj = pjs.next() if bank is None else bank
            for c in range(8):
                fw.op("pe", lambda e, c=c: e.matmul(pj[:, 0:ntok], lhsT=wt[:, c, col0:col0 + 128], rhs=hx_[:, c, 0:ntok],
                                                  start=(c == 0), stop=(c == 7)),
                      reads=[b_wt, b_hx_], writes=[b_pj], skip_self=True)
            return pj, b_pj

        def out_proj_tile(ogT, b_ogT, npair, wout, b_wout, i, gi, xres=None, b_xres=None, ytmp=None, b_ytmp=None, bank=None):
            for nh in range(2):
                pj, b_pj = pjs.next() if bank is None else bank
                for pr in range(npair):
                    fw.op("pe", lambda e, pr=pr, pj=pj: e.matmul(pj[:, :], lhsT=ogT[:, pr, i * 128:(i + 1) * 128],
                                                               rhs=wout[:, pr, nh * 512:(nh + 1) * 512],
                                                               start=(pr == 0), stop=(pr == npair - 1)),
                          reads=[b_ogT, b_wout], writes=[b_pj], skip_self=True)
                if gi is None:
                    fw.op("dve", lambda e, pj=pj, nh=nh: e.tensor_tensor(out=xres[:, nh * 512:(nh + 1) * 512], in0=pj[:, :],
                                                                       in1=xres[:, nh * 512:(nh + 1) * 512], op=ALU.add),
                          reads=[b_pj, b_xres], writes=[b_xres])
                else:
                    fw.op("dve", lambda e, pj=pj, nh=nh: e.tensor_tensor(out=ytmp[:, nh * 512:(nh + 1) * 512], in0=pj[:, :],
                                                                       in1=gB[:, gi, nh * 512:(nh + 1) * 512], op=ALU.mult),
                          reads=[b_pj, b_gB], writes=[b_ytmp])

        def fold_gate(wout, b_wout, npair, gi):
            for pr in range(npair):
                fw.op("dve", lambda e, pr=pr: e.tensor_tensor(out=wout[:, pr, :], in0=wout[:, pr, :], in1=gB[:, gi, :], op=ALU.mult),
                      reads=[b_wout, b_gB], writes=[b_wout])

        with ExitStack() as l0:
            if stop_after == "p0":
                fw.barrier()
                return nc
            KT, b_KT = sb(l0, "KT", [128, 2, 4352], BF16)
            Vaug, b_V = sb(l0, "Vaug", [128, 34, 4, 128], BF16)
            cs, b_cs = sb(l0, "cs", [128, 2, 512], F32)
            QTp = [sb(l0, "QTp%d" % i, [128, 512], BF16) for i in range(2)]
            zTp = [sb(l0, "zTp%d" % i, [128, 512], BF16) for i in range(2)]
            ogTs = [sb(l0, "ogT%d" % i, [128, 8, 512], BF16) for i in range(2)]
            hxs = [(hxT, b_hxT), sb(l0, "hxT2", [128, 8, 512], BF16)]
            sq, b_sq = sb(l0, "sq", [128, 512], BF16)
            qh, b_qh = sb(l0, "qh", [128, 512], BF16)
            rs2, b_rs2 = sb(l0, "rs2", [128, 512], F32)
            t2, b_t2 = sb(l0, "t2", [128, 512], BF16)
            ocp, b_ocp = sb(l0, "ocp", [128, 512], F32)
            s_cs = fw.new_sem("d_cs")

            SK = ""
            if "M" not in SK:
                fw.op("pool", lambda e: e.memset(Vaug[:], 1.0), writes=[b_V])

            def load_cs(tok0, ntok):
                fw.dma("sp", s_cs, cs[:, 0, 0:ntok], ropeC_d[:, tok0:tok0 + ntok], writes=[b_cs])
                fw.dma("sp", s_cs, cs[:, 1, 0:ntok], ropeS_d[:, tok0:tok0 + ntok], writes=[b_cs])

            def qk_norm_rope(pj, b_pj, ntok, gcol, dst, b_dst, rope):
                fw.op("act", lambda e: e.activation(out=sq[:, 0:ntok], in_=pj[:, 0:ntok], func=AF.Square),
                      reads=[b_pj], writes=[b_sq])
                pj2, b_pj2 = pjs.next()
                fw.op("pe", lambda e: e.matmul(pj2[:, 0:ntok], lhsT=blk1[:], rhs=sq[:, 0:ntok], start=True, stop=True),
                      reads=[b_blk1, b_sq], writes=[b_pj2], skip_self=True)
                fw.op("act", lambda e: e.activation(out=rs2[:, 0:ntok], in_=pj2[:, 0:ntok], func=AF.Ln, bias=eps_t[:, 0:1], scale=1.0 / 64),
                      reads=[b_pj2, b_eps], writes=[b_rs2])
                fw.op("act", lambda e: e.activation(out=rs2[:, 0:ntok], in_=rs2[:, 0:ntok], func=AF.Exp, scale=-0.5),
                      reads=[b_rs2], writes=[b_rs2])
                tgt, b_tgt = (qh[:, 0:ntok], b_qh) if rope else (dst, b_dst)
                fw.op("dve", lambda e: e.scalar_tensor_tensor(out=tgt, in0=pj[:, 0:ntok], scalar=Gqk[:, gcol:gcol + 1],
                                                            in1=rs2[:, 0:ntok], op0=ALU.mult, op1=ALU.mult),
                      reads=[b_pj, b_rs2, b_Gqk], writes=[b_tgt])
                if rope:
                    pj3, b_pj3 = pjs.next()
                    fw.op("pe", lambda e: e.matmul(pj3[:, 0:ntok], lhsT=RTm[:], rhs=qh[:, 0:ntok], start=True, stop=True),
                          reads=[b_RT, b_qh], writes=[b_pj3], skip_self=True)
                    fw.op("pool", lambda e: e.tensor_tensor(out=dst, in0=qh[:, 0:ntok], in1=cs[:, 0, 0:ntok], op=ALU.mult),
                          reads=[b_qh, b_cs], writes=[b_dst])
                    fw.op("dve", lambda e: e.tensor_tensor(out=t2[:, 0:ntok], in0=pj3[:, 0:ntok], in1=cs[:, 1, 0:ntok], op=ALU.mult),
                          reads=[b_pj3, b_cs], writes=[b_t2])
                    fw.op("pool", lambda e: e.tensor_tensor(out=dst, in0=dst, in1=t2[:, 0:ntok], op=ALU.add),
                          reads=[b_dst, b_t2], writes=[b_dst])

            def l0_kv_units(ntok, koff, rope, tok0, hx, b_hx):
                if rope:
                    load_cs(tok0, ntok)
                for kvp in range(2):
                    pj, b_pj = proj_fm(w0, b_w0, 1024 + kvp * 128, ntok, hx, b_hx)
                    qk_norm_rope(pj, b_pj, ntok, 1, KT[:, kvp, koff:koff + ntok], b_KT, rope)
                    yield
                for i in range(ntok // 128):
                    pj, b_pj = pjs.next()
                    for c in range(8):
                        fw.op("pe", lambda e, c=c, pj=pj: e.matmul(pj[:, 0:256], lhsT=hx[:, c, i * 128:(i + 1) * 128],
                                                                 rhs=w0[:, c, 1280:1536], start=(c == 0), stop=(c == 7)),
                              reads=[b_hx, b_w0], writes=[b_pj], skip_self=True)
                    ch = koff // 128 + i
                    for kv in range(4):
                        e_ = kv % 2
                        dst = Vaug[:, ch, kv, 64 * e_:64 * e_ + 64]
                        fw.op("dve", lambda e, kv=kv, dst=dst, pj=pj: e.tensor_copy(out=dst, in_=pj[:, kv * 64:(kv + 1) * 64]),
                              reads=[b_pj], writes=[b_V], skip_self=True, accum_w=True)
                    yield

            def norm_group_gen(ntok, which, src, hx, b_hx):
                for t in range(ntok // 128):
                    for _ in norm_T_tile_gen(t, 0, which, src_dram=src, hx=hx, b_hx=b_hx):
                        yield
                    yield

            Pb = Os.items[1]
            Oacc, b_Oacc = Os.items[0]

            def q_norm_rope_1b(ntok, rope, hx, b_hx, pr, dst, b_dst):
                pj, b_pj = proj_fm(w0, b_w0, pr * 128, ntok, hx, b_hx, bank=Pb)
                fw.op("act", lambda e: e.activation(out=sq[:, 0:ntok], in_=pj[:, 0:ntok], func=AF.Square),
                      reads=[b_pj], writes=[b_sq])
                fw.op("dve", lambda e: e.tensor_scalar(out=qh[:, 0:ntok], in0=pj[:, 0:ntok], scalar1=Gqk[:, 0:1], scalar2=None,
                                                     op0=ALU.mult),
                      reads=[b_pj, b_Gqk, b_sq], writes=[b_qh])
                yield
                fw.op("pe", lambda e: e.matmul(pj[:, 0:ntok], lhsT=blk1[:], rhs=sq[:, 0:ntok], start=True, stop=True),
                      reads=[b_blk1, b_sq], writes=[b_pj], skip_self=True)
                fw.op("act", lambda e: e.activation(out=rs2[:, 0:ntok], in_=pj[:, 0:ntok], func=AF.Ln, bias=eps_t[:, 0:1], scale=1.0 / 64),
                      reads=[b_pj, b_eps], writes=[b_rs2])
                fw.op("act", lambda e: e.activation(out=rs2[:, 0:ntok], in_=rs2[:, 0:ntok], func=AF.Exp, scale=-0.5),
                      reads=[b_rs2], writes=[b_rs2])
                tgt, b_tgt = (qh[:, 0:ntok], b_qh) if rope else (dst, b_dst)
                fw.op("dve", lambda e: e.tensor_tensor(out=tgt, in0=qh[:, 0:ntok], in1=rs2[:, 0:ntok], op=ALU.mult),
                      reads=[b_qh, b_rs2], writes=[b_tgt])
                if rope:
                    yield
                    fw.op("pe", lambda e: e.matmul(pj[:, 0:ntok], lhsT=RTm[:], rhs=qh[:, 0:ntok], start=True, stop=True),
                          reads=[b_RT, b_qh], writes=[b_pj], skip_self=True)
                    fw.op("pool", lambda e: e.tensor_tensor(out=dst, in0=qh[:, 0:ntok], in1=cs[:, 0, 0:ntok], op=ALU.mult),
                          reads=[b_qh, b_cs], writes=[b_dst])
                    fw.op("dve", lambda e: e.tensor_tensor(out=t2[:, 0:ntok], in0=pj[:, 0:ntok], in1=cs[:, 1, 0:ntok], op=ALU.mult),
                          reads=[b_pj, b_cs], writes=[b_t2])
                    fw.op("pool", lambda e: e.tensor_tensor(out=dst, in0=dst, in1=t2[:, 0:ntok], op=ALU.add),
                          reads=[b_dst, b_t2], writes=[b_dst])

            def chain_gens(gens):
                for g_ in gens:
                    for _ in g_:
                        yield

            def one_tile(gen, nsub=2):
                for _ in range(nsub):
                    try:
                        next(gen)
                    except StopIteration:
                        return
                    yield

            def l0_prepG(g, hx, b_hx):
                for t in range(4):
                    for _ in norm_T_tile_gen(t, 0, 0, src_dram=x_d[g * 512:(g + 1) * 512, :], hx=hx, b_hx=b_hx):
                        yield
                    yield

            def l0_prepP(pr, ntok, rope, tok0, hx, b_hx, QTb, b_QTb, zTb, b_zTb, first_pair):
                if rope and first_pair:
                    load_cs(tok0, ntok)
                for _ in q_norm_rope_1b(ntok, rope, hx, b_hx, pr, QTb[:, 0:ntok], b_QTb):
                    yield
                yield
                pj, b_pj = proj_fm(w0, b_w0, 1536 + pr * 128, ntok, hx, b_hx, bank=Pb)
                silu_evac(pj, b_pj, ntok, zTb[:, 0:ntok], b_zTb, scratch=(rs2, b_rs2))
                yield

            def l0_epi(g, og, b_og):
                for i in range(4):
                    ti = g * 4 + i
                    xres, b_xres, s_xres = xress.next()
                    fw.dma("sp", s_xres, xres[:], x_d[ti * 128:(ti + 1) * 128, :], writes=[b_xres])
                    yield
                    out_proj_tile(og, b_og, 8, wout0, b_wout0, i, None, xres, b_xres, bank=Pb)
                    fw.dma("sp", s_xres, x1_d[ti * 128:(ti + 1) * 128, :], xres[:], reads=[b_xres], writes=[b_x1[ti]])
                    yield

            def l0_attnP(pr, ntok, chunks, QTb, b_QTb, zTb, b_zTb, og, b_og, bgq):
                npair = len(chunks) // 2
                step = 0
                for e_ in range(2):
                    kvp = pr // 4
                    kv = 2 * kvp + e_
                    me = slice(64 * e_, 64 * e_ + 64)
                    oth = slice(64 * (1 - e_), 64 * (1 - e_) + 64)
                    O, b_O = Oacc, b_Oacc
                    pend = None
                    for jp in range(npair):
                        S2, bS = S2s.next()
                        cs2 = (chunks[2 * jp], chunks[2 * jp + 1])
                        for k, s_ in enumerate(cs2):
                            fw.op("pe", lambda e, k=k, s_=s_, S2=S2: e.matmul(
                                S2[:, k * 512:k * 512 + ntok], lhsT=KT[me, kvp, s_ * 128:(s_ + 1) * 128],
                                rhs=QTb[me, 0:ntok], start=True, stop=True),
                                reads=[b_KT, b_QTb], writes=[bS[k]], skip_self=True)
                        PT2, b_PT2 = PT2s.next()
                        if ntok == 512:
                            i_ap, o_ap = S2[:, :], PT2[:, :]
                        else:
                            i_ap = S2[:, :].rearrange("p (a b) -> p a b", b=512)[:, :, 0:ntok]
                            o_ap = PT2[:, :].rearrange("p (a b) -> p a b", b=512)[:, :, 0:ntok]
                        fw.op("act", lambda e, i_ap=i_ap, o_ap=o_ap: e.activation(out=o_ap, in_=i_ap, func=AF.Exp),
                              reads=bS, writes=[b_PT2])
                        if pend is not None:
                            pend()
                        if NFILL and ntok == 512:
                            for _f in range(NFILLN):
                                fw.op("pe", lambda e: e.matmul(Fb[:, 0:NFILL], lhsT=ident[:], rhs=hxT[:, 0, 0:NFILL],
                                                             start=True, stop=True),
                                      reads=[b_ident], writes=[b_Fb], skip_self=True)

                        def pv(jp=jp, cs2=cs2, PT2=PT2, b_PT2=b_PT2):
                            for k, s_ in enumerate(cs2):
                                fw.op("pe", lambda e, k=k, s_=s_: e.matmul(
                                    O[:, 0:ntok], lhsT=Vaug[:, s_, kv, :], rhs=PT2[:, k * 512:k * 512 + ntok],
                                    start=(jp == 0 and k == 0), stop=(jp == npair - 1 and k == 1)),
                                    reads=[b_V, b_PT2], writes=[b_O], skip_self=True)
                        pend = pv
                        if bgq is not None and step % BG_EVERY == 1:
                            next(bgq, None)
                        step += 1
                    pend()
                    fw.op("dve", lambda e: e.tensor_copy(out=ocp[:, 0:ntok], in_=O[:, 0:ntok]), reads=[b_O], writes=[b_ocp])
                    fw.op("dve", lambda e: e.reciprocal(out=rden[me, 0:ntok], in_=ocp[oth, 0:ntok]), reads=[b_ocp], writes=[b_rden])
                    fw.op("dve", lambda e: e.tensor_tensor(out=otmp[me, 0:ntok], in0=ocp[me, 0:ntok], in1=rden[me, 0:ntok], op=ALU.mult),
                          reads=[b_ocp, b_rden], writes=[b_otmp])
                    fw.op("pool", lambda e: e.tensor_tensor(out=og[me, pr, 0:ntok], in0=otmp[me, 0:ntok], in1=zTb[me, 0:ntok],
                                                           op=ALU.mult),
                          reads=[b_otmp, b_zTb], writes=[b_og])

            kv_groups = [(LC, 0, False, 0, ctx_d, 1)] + [
                (512, LC + 512 * g, True, 512 * g, x_d[g * 512:(g + 1) * 512, :], 0)
                for g in range(0 if stop_after in ("l0n", "l0k", "l0v") else 8)]
            for _ in norm_group_gen(kv_groups[0][0], kv_groups[0][5], kv_groups[0][4], *hxs[0]):
                pass
            for gi_, (ntok_, koff_, rope_, tok0_, src_, which_) in enumerate(kv_groups):
                nxt = None
                if gi_ + 1 < len(kv_groups):
                    n2 = kv_groups[gi_ + 1]
                    nxt = norm_group_gen(n2[0], n2[5], n2[4], *hxs[(gi_ + 1) % 2])
                for _ in l0_kv_units(ntok_, koff_, rope_, tok0_, *hxs[gi_ % 2]):
                    if nxt is not None:
                        next(nxt, None)
                        next(nxt, None)
                if nxt is not None:
                    for _ in nxt:
                        pass
            if debug:
                fw.dma("sp", s_c[5], dbg_kt[:, :, :], KT[:], reads=[b_KT])
                fw.dma("sp", s_c[6], dbg_v[:, :, :, :], Vaug[:], reads=[b_V])
            if stop_after not in ("l0kv", "l0n", "l0k", "l0v"):
                ogc, b_ogc = ogTs[1]
                norm_T(2, 0, 1, src_dram=ctx_d)
                for pr in range(8):
                    QTb, b_QTb = QTp[pr % 2]
                    zTb, b_zTb = zTp[pr % 2]
                    import os
                    BX = os.environ.get("BX", "")
                    if BX == "skip":
                        continue
                    for _iu, _ in enumerate(l0_prepP(pr, LC, False, 0, hxT, b_hxT, QTb, b_QTb, zTb, b_zTb, False)):
                        if BX == "q" :
                            break
                    if BX in ("q", "qz"):
                        continue
                    l0_attnP(pr, LC, [0, 1], QTb, b_QTb, zTb, b_zTb, ogc, b_ogc, None)
                for i in range(2):
                    xres, b_xres, s_xres = xress.next()
                    ytmp, b_ytmp, _s = xins.next()
                    fw.dma("sp", s_xres, xres[:], ctx_d[i * 128:(i + 1) * 128, :], writes=[b_xres])
                    out_proj_tile(ogc, b_ogc, 8, wout0, b_wout0, i, 1, ytmp=ytmp, b_ytmp=b_ytmp)
                    fw.op("pool", lambda e, i=i, xres=xres, ytmp=ytmp: e.tensor_tensor(out=ctx1[:, i, :], in0=xres[:], in1=ytmp[:], op=ALU.add),
                          reads=[b_xres, b_ytmp], writes=[b_ctx1])
                fold_gate(wout0, b_wout0, 8, 0)
                if debug:
                    fw.dma("sp", s_c[7], dbg_ctx1[:, :, :], ctx1[:], reads=[b_ctx1])
                ng = 8 if stop_after is None or not stop_after.startswith("l0g") else int(stop_after[3:])
                import os
                if os.environ.get("FAST"):
                    ng = 0
                allch = list(range(34))
                if ng > 0:
                    for _ in l0_prepG(0, *hxs[0]):
                        pass
                    for _ in l0_prepP(0, 512, True, 0, *hxs[0], *QTp[0], *zTp[0], True):
                        pass
                k = 0
                epi = None
                for g in range(ng):
                    og, b_og = ogTs[g % 2]
                    genG = None
                    for pr in range(8):
                        gens = []
                        genP = None
                        if pr < 7:
                            genP = l0_prepP(pr + 1, 512, True, 512 * g, *hxs[g % 2], *QTp[(k + 1) % 2], *zTp[(k + 1) % 2], False)
                        elif g + 1 < ng:
                            genP = l0_prepP(0, 512, True, 512 * (g + 1), *hxs[(g + 1) % 2], *QTp[(k + 1) % 2], *zTp[(k + 1) % 2], True)
                        if genP is not None:
                            gens.append(genP)
                        if pr < 4 and epi is not None:
                            gens.append(one_tile(epi))
                        if pr == 3 and g + 1 < ng:
                            genG = l0_prepG(g + 1, *hxs[(g + 1) % 2])
                        if 3 <= pr <= 6 and genG is not None:
                            gens.append(one_tile(genG, 3))
                        bgq = chain_gens(gens)
                        QTb, b_QTb = QTp[k % 2]
                        zTb, b_zTb = zTp[k % 2]
                        l0_attnP(pr, 512, allch, QTb, b_QTb, zTb, b_zTb, og, b_og, bgq)
                        for _ in bgq:
                            pass
                        k += 1
                    if epi is not None:
                        for _ in epi:
                            pass
                    if genG is not None:
                        for _ in genG:
                            pass
                    epi = l0_epi(g, og, b_og)
                if epi is not None:
                    for _ in epi:
                        pass
            fw.barrier()
        l0w.close()

        if stop_after is None:
            with ExitStack() as l1:
                w1, b_w1 = sb(l1, "w1", [128, 8, 2048], BF16)
                wout1, b_wout1 = sb(l1, "wout1", [128, 4, D], BF16)
                tbs, b_tb = sb(l1, "tbs", [128, NM, 8, 128], BF16)
                KTr, b_KTr = sb(l1, "KTr", [128, 4, 16 * 128], BF16)
                KTc, b_KTc = sb(l1, "KTc", [128, 4, LC], BF16)
                Vr, b_Vr = sb(l1, "Vr", [128, 16, 4, 2, 128], BF16)
                Vc, b_Vc = sb(l1, "Vc", [128, 2, 4, 2, 128], BF16)
                QTs = [sb(l1, "QT1_%d" % i, [128, 4, 512], BF16) for i in range(3)]
                zTs = [sb(l1, "zT1_%d" % i, [128, 4, 512], BF16) for i in range(3)]
                og1s = [sb(l1, "og1_%d" % i, [128, 4, 512], BF16) for i in range(2)]
                s_m = fw.new_sem("d_rmask")
                s_tb = fw.new_sem("d_tb")

                fw.dma("sp", s_c[3], gB[:, 0, :], bass.AP(fg_d.tensor, 0, [[0, 128], [1, D]]), writes=[b_gB])
                fw.op("pool", lambda e: e.memset(Vr[:], 1.0), writes=[b_Vr])
                fw.op("pool", lambda e: e.memset(Vc[:], 1.0), writes=[b_Vc])

                def l1_v_evac(pj, b_pj, dstV, b_dst, ch):
                    pjv = pj[:, :].rearrange("p (a b c) -> p a b c", b=2, c=64)
                    fw.op("dve", lambda e: e.tensor_copy(out=dstV[:, ch, :, 0, 0:64], in_=pjv[:, :, 0, :]),
                          reads=[b_pj], writes=[b_dst], skip_self=True, accum_w=True)
                    fw.op("dve", lambda e: e.tensor_copy(out=dstV[:, ch, :, 1, 64:128], in_=pjv[:, :, 1, :]),
                          reads=[b_pj], writes=[b_dst], skip_self=True, accum_w=True)

                def l1_stageA(ntok, is_ctx, g, src_dram=None, src_sbuf=None, src_bufs=None):
                    for t in range(ntok // 128):
                        for _ in norm_T_tile_gen(t, 1, 1 if is_ctx else 0, src_dram, src_sbuf, src_bufs):
                            yield
                        yield
                    slot = 0 if is_ctx else g % 4
                    for pl in range(4):
                        pj, b_pj = proj_fm(w1, b_w1, 512 + pl * 128, ntok)
                        dst, b_dst = (KTc[:, pl, 0:ntok], b_KTc) if is_ctx else (KTr[:, pl, slot * 512:slot * 512 + ntok], b_KTr)
                        fw.op("dve", lambda e, pj=pj, dst=dst: e.tensor_copy(out=dst, in_=pj[:, 0:ntok]),
                              reads=[b_pj], writes=[b_dst], skip_self=True, accum_w=True)
                        yield
                    for i in range(ntok // 128):
                        pj, b_pj = pjs.next()
                        for c in range(8):
                            fw.op("pe", lambda e, c=c, pj=pj: e.matmul(pj[:, :], lhsT=hxT[:, c, i * 128:(i + 1) * 128],
                                                                     rhs=w1[:, c, 1024:1536], start=(c == 0), stop=(c == 7)),
                                  reads=[b_hxT, b_w1], writes=[b_pj], skip_self=True)
                        if is_ctx:
                            l1_v_evac(pj, b_pj, Vc, b_Vc, i)
                        else:
                            l1_v_evac(pj, b_pj, Vr, b_Vr, slot * 4 + i)
                        yield
                    if not is_ctx:
                        QT1, b_QT1 = QTs[g % 3]
                        zT1, b_zT1 = zTs[g % 3]
                        for pl in range(4):
                            pj, b_pj = proj_fm(w1, b_w1, pl * 128, ntok)
                            fw.op("dve", lambda e, pj=pj, pl=pl: e.tensor_scalar(out=QT1[:, pl, :], in0=pj[:, :], scalar1=0.125,
                                                                              scalar2=None, op0=ALU.mult),
                                  reads=[b_pj], writes=[b_QT1], skip_self=True, accum_w=True)
                            yield
                        for pl in range(4):
                            pj, b_pj = proj_fm(w1, b_w1, 1536 + pl * 128, ntok)
                            silu_evac(pj, b_pj, ntok, zT1[:, pl, :], b_zT1)
                            yield

                def drain(gen):
                    if gen is not None:
                        for _ in gen:
                            pass

                def l1_stageB(g, p, gen=None, epi=None):
                    og1, b_og1 = og1s[g % 2]
                    nblk = [0]
                    QT1, b_QT1 = QTs[g % 3]
                    zT1, b_zT1 = zTs[g % 3]
                    for qi in range(4):
                        qb = 4 * g + qi
                        qs = slice(qi * 128, (qi + 1) * 128)
                        klist = [(m, True) for m in chunks_of[qb]] + [(0, False), (1, False)]
                        n = len(klist)
                        for hq in range(2):
                            O, b_O = Os.next()
                            pendq = []
                            for j, (m, nb) in enumerate(klist):
                                S, b_S = Ss.next()
                                import os
                                NOB = os.environ.get("L1B", "")
                                if nb:
                                    di = m - qb + 3
                                    mi = mask_of[(qb, m)]
                                    ridx = ((m // 4) % 4) * 4 + (m % 4)
                                    fw.op("pe", lambda e, S=S, mi=mi: e.matmul(
                                        S[:, :], lhsT=ident[:], rhs=tbs[:, mi, 4 * hq:4 * hq + 4, :].rearrange("p a b -> p (a b)"),
                                        start=True, stop=False), reads=[b_ident, b_tb], writes=[b_S], skip_self=True)
                                for hi in range(4):
                                    e_ = hq
                                    pl = hi
                                    me = slice(64 * e_, 64 * e_ + 64)
                                    if nb:
                                        lhs = KTr[me, pl, ridx * 128:(ridx + 1) * 128]
                                        rb = b_KTr
                                    else:
                                        lhs = KTc[me, pl, m * 128:(m + 1) * 128]
                                        rb = b_KTc
                                    fw.op("pe", lambda e, S=S, hi=hi, lhs=lhs, me=me, pl=pl: e.matmul(
                                        S[:, hi * 128:(hi + 1) * 128], lhsT=lhs, rhs=QT1[me, pl, qs],
                                        start=((not nb) and hi == 0), stop=(hi == 3)), reads=[rb, b_QT1], writes=[b_S], skip_self=True)
                                PT, b_PT = PTs.next()
                                if NOB == "qkonly":
                                    continue
                                fw.op("act", lambda e, S=S, PT=PT: e.activation(out=PT[:, :], in_=S[:, :], func=AF.Exp),
                                      reads=[b_S], writes=[b_PT])
                                if NOB == "nopv":
                                    continue
                                if len(pendq) >= 2:
                                    pendq.pop(0)()
                                if j >= 1:
                                    nblk[0] += 1
                                    if epi is not None and nblk[0] % 3 == 0:
                                        next(epi, None)
                                    elif gen is not None:
                                        next(gen, None)

                                def pv(j=j, m=m, nb=nb, PT=PT, b_PT=b_PT, O=O, b_O=b_O):
                                    for hi in range(4):
                                        if nb:
                                            ridx = ((m // 4) % 4) * 4 + (m % 4)
                                            lhs = Vr[:, ridx, hi, hq, :]
                                            rb = b_Vr
                                        else:
                                            lhs = Vc[:, m, hi, hq, :]
                                            rb = b_Vc
                                        fw.op("pe", lambda e, hi=hi, lhs=lhs: e.matmul(
                                            O[:, hi * 128:(hi + 1) * 128], lhsT=lhs, rhs=PT[:, hi * 128:(hi + 1) * 128],
                                            start=(j == 0 and hi == 0), stop=(j == n - 1 and hi == 3)), reads=[rb, b_PT], writes=[b_O], skip_self=True)
                                pendq.append(pv)
                            for f_ in pendq:
                                f_()
                            import os
                            L1T = os.environ.get("L1T", "")
                            if L1T == "b1":
                                continue
                            Ov = O[:, :].rearrange("p (a c) -> p a c", c=128)
                            rdv = rden[:, :].rearrange("p (a c) -> p a c", c=128)
                            otv = otmp[:, :].rearrange("p (a c) -> p a c", c=128)
                            me = slice(64 * hq, 64 * hq + 64)
                            oth = slice(64 * (1 - hq), 64 * (1 - hq) + 64)
                            fw.op("dve", lambda e, me=me, oth=oth: e.reciprocal(out=rdv[me, :, :], in_=Ov[oth, :, :]),
                                  reads=[b_O], writes=[b_rden])
                            fw.op("dve", lambda e, me=me: e.tensor_tensor(out=otv[me, :, :], in0=Ov[me, :, :], in1=rdv[me, :, :], op=ALU.mult),
                                  reads=[b_O, b_rden], writes=[b_otmp])
                            fw.op("pool", lambda e, me=me: e.tensor_tensor(out=og1[me, :, qs], in0=otv[me, :, :], in1=zT1[me, :, qs], op=ALU.mult),
                                  reads=[b_otmp, b_zT1], writes=[b_og1])
                            nblk[0] += 1
                    if debug and p == 0 and g == 0:
                        fw.dma("sp", s_c[0], dbg_og1[:, :, :], og1[:], reads=[b_og1])
                        fw.dma("sp", s_c[1], dbg_qt1[:, :, :], QT1[:], reads=[b_QT1])
                        fw.dma("sp", s_c[2], dbg_zt1[:, :, :], zT1[:], reads=[b_zT1])
                        fw.dma("sp", s_c[3], dbg_ktr[:, :, :], KTr[:], reads=[b_KTr])
                        fw.dma("sp", s_c[4], dbg_ktc[:, :, :], KTc[:], reads=[b_KTc])
                        fw.dma("sp", s_c[5], dbg_vr[:, :, :, :, :], Vr[:], reads=[b_Vr])
                        fw.dma("sp", s_c[6], dbg_vc[:, :, :, :, :], Vc[:], reads=[b_Vc])

                def l1_epilogue(g, p):
                    og1, b_og1 = og1s[g % 2]
                    for i in range(4):
                        ti = g * 4 + i
                        xres, b_xres, s_xres = xress.next()
                        if p == 0:
                            fw.dma("sp", s_xres, xres[:], x1_d[ti * 128:(ti + 1) * 128, :], reads=[b_x1[ti]], writes=[b_xres])
                        else:
                            fw.dma("sp", s_xres, xres[:], x1b_d[ti * 128:(ti + 1) * 128, :], reads=[b_x1b[ti]], writes=[b_xres])
                        yield
                        out_proj_tile(og1, b_og1, 4, wout1, b_wout1, i, None, xres, b_xres)
                        if p == 0:
                            fw.dma("sp", s_xres, x1b_d[ti * 128:(ti + 1) * 128, :], xres[:], reads=[b_xres], writes=[b_x1b[ti]])
                        else:
                            xn, b_xn = xns.next()
                            st, b_st = sts.next()
                            yo, b_yo, s_yo = xins.next()
                            fw.op("act", lambda e, xn=xn, st=st, xres=xres: e.activation(out=xn[:], in_=xres[:], func=AF.Square,
                                                                                      accum_out=st[:, 0:1]),
                                  reads=[b_xres, b_st], writes=[b_xn, b_st])
                            fw.op("act", lambda e, st=st: e.activation(out=st[:, 1:2], in_=st[:, 0:1], func=AF.Ln, bias=eps_t[:, 0:1],
                                                                     scale=1.0 / D),
                                  reads=[b_st, b_eps], writes=[b_st])
                            fw.op("act", lambda e, st=st: e.activation(out=st[:, 1:2], in_=st[:, 1:2], func=AF.Exp, scale=-0.5),
                                  reads=[b_st], writes=[b_st])
                            fw.op("dve", lambda e, st=st, xres=xres, yo=yo: e.scalar_tensor_tensor(
                                out=yo[:], in0=xres[:], scalar=st[:, 1:2], in1=gB[:, 0, :], op0=ALU.mult, op1=ALU.mult),
                                reads=[b_xres, b_st, b_gB], writes=[b_yo])
                            fw.dma("sp", s_yo, out_d[ti * 128:(ti + 1) * 128, :], yo[:], reads=[b_yo])
                        yield

                for p in range(2):
                    for c in range(8):
                        fw.dma("pool", s_w[c % 2], w1[:, c, :], w1_d[p, c * 128:(c + 1) * 128, :], writes=[b_w1])
                    for pl in range(4):
                        r0_ = 512 * p + pl * 128
                        fw.dma("pool", s_w[2 + pl % 2], wout1[:, pl, :], wout1_d[r0_:r0_ + 128, :], writes=[b_wout1])
                    for di in range(NM):
                        fw.dma("pool", s_tb, tbs[:, di, :, :], tb_d[p, :, di, :, :], writes=[b_tb])
                    import os
                    L1T = os.environ.get("L1T", "")
                    drain(l1_stageA(LC, True, 0, src_sbuf=ctx1))
                    drain(l1_stageA(512, False, 0, src_dram=x1_d[0:512, :], src_bufs=b_x1[0:4]))
                    drain(l1_stageA(512, False, 1, src_dram=x1_d[512:1024, :], src_bufs=b_x1[4:8]))
                    fold_gate(wout1, b_wout1, 4, 2)
                    epi = None
                    for g in range(8):
                        gen = None
                        if g + 2 < 8:
                            gen = l1_stageA(512, False, g + 2, src_dram=x1_d[(g + 2) * 512:(g + 3) * 512, :],
                                            src_bufs=b_x1[(g + 2) * 4:(g + 3) * 4])
                        l1_stageB(g, p, gen, epi)
                        drain(gen)
                        drain(epi)
                        epi = l1_epilogue(g, p)
                    drain(epi)
                fw.barrier()
        fw.barrier()
    return nc


_CACHE = {}


def kernel(**inputs):
    per_core, meta = _prep(inputs)
    nc = build(meta)
    res = run_bass_kernel_spmd(nc, per_core, core_ids=list(range(len(per_core))))
    out = np.stack([np.asarray(r["out"], dtype=np.float32) for r in res.results], axis=0)
    return out
```
